# Optimizing a Trainium2 kernel written in Bass

```python
import jax, jax.numpy as jnp
from jax import lax
import numpy as np

D_MODEL = 2048
BATCH = 16
SEQ = 256
DEPTH = 2
DEC_BATCH = 8
DEC_SEQ = 4096
PAST_LEN = 256

GRID_W = 64
H_A = 16
HEAD_A = 64
W_A = H_A * HEAD_A
H_B = 16
W_B = 1024
BLK_B = W_B // H_B
CONV_W = 4
CONV_PAD_L = 2
CONV_PAD_R = CONV_W - 1 - CONV_PAD_L
LORA_DEC = 96
LORA_ICLR = 96
LORA_VRES = 64
LRU_C = 8.0
RMS_EPS = 1e-6
GN_EPS = 64e-5
N_IN = 4 * W_A + 2 * W_B + 2 * D_MODEL
SPLITS = (W_A, 2 * W_A, 3 * W_A, 4 * W_A, 4 * W_A + W_B, 4 * W_A + 2 * W_B, 4 * W_A + 2 * W_B + D_MODEL)

kernel_name = 'bidir_rwkv7_rglru_prefix_diffusion_step'


def rms_norm(x, g):
    xf = x.astype(jnp.float32)
    y = xf * lax.rsqrt(jnp.mean(xf * xf, axis=-1, keepdims=True) + RMS_EPS)
    return (y * g.astype(jnp.float32)).astype(x.dtype)


def to_heads(t):
    return t.reshape(t.shape[0], t.shape[1], H_A, HEAD_A)


def shift_1d(z):
    zp = jnp.pad(z, ((0, 0), (1, 1), (0, 0)))
    return 0.5 * (zp[:, :-2] + zp[:, 2:])


def shift_grid(z):
    bsz, n, ch = z.shape
    rows = n // GRID_W
    zp = jnp.pad(z.reshape(bsz, rows, GRID_W, ch), ((0, 0), (1, 1), (1, 1), (0, 0)))
    s = 0.25 * (zp[:, :-2, 1:-1] + zp[:, 2:, 1:-1] + zp[:, 1:-1, :-2] + zp[:, 1:-1, 2:])
    return s.reshape(bsz, n, ch)


def dwconv_centred(x, w, b):
    t = x.shape[1]
    xp = jnp.pad(x, ((0, 0), (CONV_PAD_L, CONV_PAD_R), (0, 0)))
    y = b + xp[:, 0:t] * w[0]
    for j in range(1, CONV_W):
        y = y + xp[:, j:j + t] * w[j]
    return y


def head_group_norm(y, g, b):
    bsz, t = y.shape[0], y.shape[1]
    yh = y.astype(jnp.float32).reshape(bsz, t, H_A, HEAD_A)
    mu = jnp.mean(yh, axis=-1, keepdims=True)
    var = jnp.mean(jnp.square(yh - mu), axis=-1, keepdims=True)
    yh = (yh - mu) * lax.rsqrt(var + GN_EPS)
    return yh.reshape(bsz, t, W_A) * g.astype(jnp.float32) + b.astype(jnp.float32)


def rwkv7_scan(r, w, k, v, kk, a, s0, reverse):
    def step(s, inp):
        r_t, w_t, k_t, v_t, kk_t, a_t = inp
        s_kk = jnp.einsum('bhvk,bhk->bhv', s, kk_t)
        s = s * w_t[:, :, None, :] - s_kk[..., None] * (kk_t * a_t)[:, :, None, :] + v_t[..., None] * k_t[:, :, None, :]
        return s, jnp.einsum('bhvk,bhk->bhv', s, r_t)
    xs = tuple(jnp.moveaxis(t.astype(jnp.float32), 1, 0) for t in (r, w, k, v, kk, a))
    s_fin, ys = lax.scan(step, s0.astype(jnp.float32), xs, reverse=reverse)
    return jnp.moveaxis(ys, 0, 1), s_fin


def linear_scan(a, u, h0, reverse):
    def step(h, au):
        h = au[0] * h + au[1]
        return h, h
    h_fin, hs = lax.scan(step, h0.astype(jnp.float32), (jnp.moveaxis(a, 1, 0), jnp.moveaxis(u, 1, 0)), reverse=reverse)
    return jnp.moveaxis(hs, 0, 1), h_fin


def rwkv_direction(xm, rh, k, vh, kk, dec_w0, dec_w1, dec_w2, iclr_w0, iclr_w1, iclr_w2, k_a, r_k, s0, reverse):
    w_log = -jax.nn.softplus(-(dec_w0 + jnp.tanh(xm @ dec_w1) @ dec_w2)) - 0.5
    decay = jnp.exp(-jnp.exp(w_log.astype(jnp.float32)))
    a = jax.nn.sigmoid(iclr_w0 + (xm @ iclr_w1) @ iclr_w2)
    kd = k * (1 + (a - 1) * k_a)
    kdh = to_heads(kd)
    y, s_fin = rwkv7_scan(rh, to_heads(decay), kdh, vh, kk, to_heads(a), s0, reverse)
    bonus = jnp.sum((rh * kdh * r_k).astype(jnp.float32), axis=-1, keepdims=True) * vh.astype(jnp.float32)
    return y, bonus, s_fin


def rglru_direction(xc, gr_w, gr_b, gi_w, gi_b, lam, h0, reverse):
    bsz, t = xc.shape[0], xc.shape[1]
    xblk = xc.reshape(bsz, t, H_B, BLK_B)
    rg = jax.nn.sigmoid(jnp.einsum('bthi,hij->bthj', xblk, gr_w).reshape(bsz, t, W_B) + gr_b)
    ig = jax.nn.sigmoid(jnp.einsum('bthi,hij->bthj', xblk, gi_w).reshape(bsz, t, W_B) + gi_b)
    log_a = (-LRU_C * jax.nn.softplus(-lam) * rg).astype(jnp.float32)
    a = jnp.exp(log_a)
    u = jnp.sqrt(-jnp.expm1(2.0 * log_a)) * (ig * xc).astype(jnp.float32)
    return linear_scan(a, u, h0, reverse)


def mixer(xm, p, shift_fn, s_rwkv0, h_lru0, v_first):
    dt = xm.dtype
    z = xm @ p['w_in']
    r, k, v, g_a, x_b, g_b, m_a, m_b = jnp.split(z, SPLITS, axis=-1)
    mu = p['mu_rkv']
    r = r + (shift_fn(r) - r) * mu[0]
    k = k + (shift_fn(k) - k) * mu[1]
    v = v + (shift_fn(v) - v) * mu[2]
    if 'vres_w0' in p:
        v = v + (v_first - v) * jax.nn.sigmoid(p['vres_w0'] + (xm @ p['vres_w1']) @ p['vres_w2'])
    rh, vh = to_heads(r), to_heads(v)
    kk = to_heads((k * p['k_k']).astype(jnp.float32))
    kk = kk / jnp.maximum(jnp.sqrt(jnp.sum(kk * kk, axis=-1, keepdims=True)), 1e-12)
    dirs_a = []
    for d in range(2):
        dirs_a.append(rwkv_direction(xm, rh, k, vh, kk, p['dec_w0'][d], p['dec_w1'][d], p['dec_w2'][d],
                                     p['iclr_w0'][d], p['iclr_w1'][d], p['iclr_w2'][d], p['k_a'], p['r_k'],
                                     s_rwkv0[:, d], d == 1))
    (y_f, bo_f, s_f), (y_bk, bo_bk, s_bk) = dirs_a
    bsz, t = xm.shape[0], xm.shape[1]
    y_rwkv = head_group_norm((y_f + y_bk).reshape(bsz, t, W_A), p['lnx_g'], p['lnx_b']) + (bo_f + bo_bk).reshape(bsz, t, W_A)
    y_a = (y_rwkv.astype(dt) * jax.nn.silu(g_a)) @ p['w_out_a']

    xc = dwconv_centred(x_b, p['conv_w'], p['conv_b'])
    dirs_b = []
    for d in range(2):
        dirs_b.append(rglru_direction(xc, p['gr_w'][d], p['gr_b'][d], p['gi_w'][d], p['gi_b'][d],
                                      p['lru_lambda'][d], h_lru0[:, d], d == 1))
    (h_f, hf_f), (h_bk, hf_bk) = dirs_b
    y_b = ((h_f + h_bk).astype(dt) * jax.nn.silu(g_b)) @ p['w_out_b']

    merged = jax.nn.sigmoid(m_a) * y_a + jax.nn.sigmoid(m_b) * y_b
    out = merged @ p['w_out']
    return out, jnp.stack([s_f, s_bk], axis=1), jnp.stack([hf_f, hf_bk], axis=1), v


def trunk(x, cond, shift_fn, rwkv_init, lru_init, layers):
    v_first = None
    s_out, h_out = [], []
    for l in range(DEPTH):
        p = layers[l]
        mod = jax.nn.silu(cond) @ p['ada_w'] + p['ada_b']
        shift, scale, gate = jnp.split(mod, 3, axis=-1)
        xm = rms_norm(x, p['norm_g']) * (1 + scale[:, None]) + shift[:, None]
        out, s_fin, h_fin, v = mixer(xm, p, shift_fn, rwkv_init[l], lru_init[l], v_first)
        if v_first is None:
            v_first = v
        x = x + gate[:, None] * out
        s_out.append(s_fin)
        h_out.append(h_fin)
    return x, jnp.stack(s_out, axis=1), jnp.stack(h_out, axis=1)


def setup_inputs(seed: int = 0) -> dict:
    key = jax.random.key(seed)
    ks = iter(jax.random.split(key, 48))
    L, D = DEPTH, D_MODEL

    def nrm(shape, s):
        return jax.random.normal(next(ks), shape, jnp.float32) * s

    def uni(shape, lo, hi):
        return jax.random.uniform(next(ks), shape, jnp.float32, lo, hi)

    inp = {}
    inp['x_prompt'] = nrm((BATCH, SEQ, D), 1.0)
    inp['x_sample'] = nrm((DEC_BATCH, DEC_SEQ, D), 1.0)
    inp['c'] = nrm((DEC_BATCH, D), 1.0)
    inp['state_rwkv'] = nrm((DEC_BATCH, L, 2, H_A, HEAD_A, HEAD_A), 0.5)
    inp['state_lru'] = nrm((DEC_BATCH, L, 2, W_B), 0.5)
    inp['c_ctx'] = nrm((D,), 1.0)
    inp['norm_g'] = 1.0 + nrm((L, D), 0.1)
    inp['ada_w'] = nrm((L, D, 3 * D), 0.5 * D ** -0.5)
    inp['ada_b'] = nrm((L, 3 * D), 0.02)
    inp['w_in'] = nrm((L, D, N_IN), D ** -0.5)
    inp['mu_rkv'] = uni((L, 3, W_A), 0.0, 1.0)
    inp['dec_w0'] = uni((L, 2, W_A), -3.0, 1.0)
    inp['dec_w1'] = nrm((L, 2, D, LORA_DEC), D ** -0.5)
    inp['dec_w2'] = nrm((L, 2, LORA_DEC, W_A), 0.5 * LORA_DEC ** -0.5)
    inp['iclr_w0'] = nrm((L, 2, W_A), 0.5)
    inp['iclr_w1'] = nrm((L, 2, D, LORA_ICLR), D ** -0.5)
    inp['iclr_w2'] = nrm((L, 2, LORA_ICLR, W_A), 0.5 * LORA_ICLR ** -0.5)
    inp['vres_w0'] = nrm((L - 1, W_A), 0.5)
    inp['vres_w1'] = nrm((L - 1, D, LORA_VRES), D ** -0.5)
    inp['vres_w2'] = nrm((L - 1, LORA_VRES, W_A), 0.5 * LORA_VRES ** -0.5)
    inp['k_k'] = 0.85 + nrm((L, W_A), 0.05)
    inp['k_a'] = 1.0 + nrm((L, W_A), 0.05)
    inp['r_k'] = nrm((L, H_A, HEAD_A), 0.1)
    inp['lnx_g'] = 1.0 + nrm((L, W_A), 0.1)
    inp['lnx_b'] = nrm((L, W_A), 0.02)
    inp['w_out_a'] = nrm((L, W_A, D), W_A ** -0.5)
    inp['conv_w'] = nrm((L, CONV_W, W_B), CONV_W ** -0.5)
    inp['conv_b'] = nrm((L, W_B), 0.02)
    inp['gr_w'] = nrm((L, 2, H_B, BLK_B, BLK_B), BLK_B ** -0.5)
    inp['gr_b'] = nrm((L, 2, W_B), 0.02)
    inp['gi_w'] = nrm((L, 2, H_B, BLK_B, BLK_B), BLK_B ** -0.5)
    inp['gi_b'] = nrm((L, 2, W_B), 0.02)
    a_root = uni((L, 2, W_B), 0.9, 0.999) ** (1.0 / LRU_C)
    inp['lru_lambda'] = jnp.log(a_root) - jnp.log1p(-a_root)
    inp['w_out_b'] = nrm((L, W_B, D), W_B ** -0.5)
    inp['w_out'] = nrm((L, D, D), D ** -0.5)
    inp['final_g'] = 1.0 + nrm((D,), 0.1)
    return inp


def reference(x_prompt, x_sample, c, state_rwkv, state_lru, c_ctx, norm_g, ada_w, ada_b, w_in, mu_rkv,
              dec_w0, dec_w1, dec_w2, iclr_w0, iclr_w1, iclr_w2, vres_w0, vres_w1, vres_w2, k_k, k_a, r_k,
              lnx_g, lnx_b, w_out_a, conv_w, conv_b, gr_w, gr_b, gi_w, gi_b, lru_lambda, w_out_b, w_out,
              final_g):
    layers = []
    for l in range(DEPTH):
        p = dict(norm_g=norm_g[l], ada_w=ada_w[l], ada_b=ada_b[l], w_in=w_in[l], mu_rkv=mu_rkv[l],
                 dec_w0=dec_w0[l], dec_w1=dec_w1[l], dec_w2=dec_w2[l], iclr_w0=iclr_w0[l],
                 iclr_w1=iclr_w1[l], iclr_w2=iclr_w2[l], k_k=k_k[l], k_a=k_a[l], r_k=r_k[l],
                 lnx_g=lnx_g[l], lnx_b=lnx_b[l], w_out_a=w_out_a[l], conv_w=conv_w[l], conv_b=conv_b[l],
                 gr_w=gr_w[l], gr_b=gr_b[l], gi_w=gi_w[l], gi_b=gi_b[l], lru_lambda=lru_lambda[l],
                 w_out_b=w_out_b[l], w_out=w_out[l])
        if l > 0:
            p['vres_w0'] = vres_w0[l - 1]
            p['vres_w1'] = vres_w1[l - 1]
            p['vres_w2'] = vres_w2[l - 1]
        layers.append(p)

    bp = x_prompt.shape[0]
    cond_ctx = jnp.broadcast_to(c_ctx, (bp, D_MODEL))
    zeros_rwkv = [jnp.zeros((bp, 2, H_A, HEAD_A, HEAD_A), jnp.float32) for _ in range(DEPTH)]
    zeros_lru = [jnp.zeros((bp, 2, W_B), jnp.float32) for _ in range(DEPTH)]
    x_ctx, new_state_rwkv, new_state_lru = trunk(x_prompt, cond_ctx, shift_1d, zeros_rwkv, zeros_lru, layers)

    rwkv_init = [state_rwkv[:, l] for l in range(DEPTH)]
    lru_init = [state_lru[:, l] for l in range(DEPTH)]
    x_lat, _, _ = trunk(x_sample, c, shift_grid, rwkv_init, lru_init, layers)

    y_prompt = rms_norm(x_ctx, final_g)
    y_sample = rms_norm(x_lat, final_g)
    return (y_prompt, y_sample, new_state_rwkv.astype(x_prompt.dtype), new_state_lru.astype(x_prompt.dtype))
```

```python
import numpy as np
from contextlib import ExitStack
import concourse.bass as bass
import concourse.mybir as mybir
from concourse.bass_utils import run_bass_kernel_spmd

F32 = mybir.dt.float32
F32R = mybir.dt.float32r
BF16 = mybir.dt.bfloat16
MMDT = BF16
AF = mybir.ActivationFunctionType
ALU = mybir.AluOpType

D = 2048
L = 2
NIN = 10240
RMS_EPS = 1e-6
GN_EPS = 64e-5
EPOCH = 30000
NSLOT = {'sp': 40, 'pool': 24}

VEC_SPEC = [('norm_g', 16), ('ada_b', 48), ('mu_r', 8), ('mu_k', 8), ('mu_v', 8), ('dec_w0_0', 8), ('dec_w0_1', 8),
            ('iclr_w0_0', 8), ('iclr_w0_1', 8), ('vres_w0', 8), ('k_k', 8), ('k_a', 8), ('r_k', 8), ('lnx_g', 8),
            ('lnx_b', 8), ('conv_w0', 8), ('conv_w1', 8), ('conv_w2', 8), ('conv_w3', 8), ('conv_b', 8),
            ('gr_b_0', 8), ('gr_b_1', 8), ('gi_b_0', 8), ('gi_b_1', 8), ('lam_0', 8), ('lam_1', 8)]
VOFF = {}
_o = 0
for _l in range(L):
    for _n, _c in VEC_SPEC:
        VOFF[(_n, _l)] = _o
        _o += _c
VOFF[('final_g', 0)] = _o
_o += 16
VOFF[('eps_rms', 0)] = _o
_o += 1
VOFF[('eps_gn', 0)] = _o
_o += 1
NVEC = _o


def pm(v):
    v = np.asarray(v, np.float32).reshape(-1, 128)
    return np.ascontiguousarray(v.T)


class Buf:
    def __init__(s, t, name, excl=False):
        s.t = t
        s.name = name
        s.w = None
        s.r = {}
        s.excl = excl

    def __getitem__(s, k):
        return V(s.t[k], [s])


class V:
    def __init__(s, ap, bufs):
        s.ap = ap
        s.bufs = bufs

    def bc(s, dt):
        return V(s.ap.bitcast(dt), s.bufs)

    def re(s, pat, **kw):
        return V(s.ap.rearrange(pat, **kw), s.bufs)

    def bcast(s, axis, shape):
        return V(s.ap.unsqueeze(axis).broadcast_to(list(shape)), s.bufs)

    def bto(s, shape):
        return V(s.ap.broadcast_to(list(shape)), s.bufs)

    def __getitem__(s, k):
        return V(s.ap[k], s.bufs)


class Sched:
    def __init__(s, nc):
        s.nc = nc
        s.ops = {e: [] for e in ('pe', 'act', 'dve', 'pool', 'sp')}
        s.cnt = {e: 0 for e in s.ops}
        s.waited = {e: {} for e in s.ops}
        s.slot = {'sp': 0, 'pool': 0}
        s.dcnt = {}
        s.keys = set()
        s.last = {}

    def _deps(s, e, r, w):
        toks = []
        for b in r:
            if b.w is not None:
                toks.append(b.w)
            if b.excl:
                toks.extend(t_ for t_ in b.r.values() if t_[2] != e)
        for b in w:
            if b.w is not None:
                toks.append(b.w)
            toks.extend(b.r.values())
        waits = []
        for (key, val, te, kind) in toks:
            if kind == 'c' and te == e and e == 'pe':
                continue
            if s.waited[e].get(key, 0) >= val:
                continue
            s.waited[e][key] = val
            waits.append((key, val))
        return waits

    def _upd(s, tok, r, w):
        for b in r:
            b.r[tok[0]] = tok
        for b in w:
            b.w = tok
            b.r = {}
        s.last[tok[0]] = tok

    def op(s, e, fn, r=(), w=()):
        waits = s._deps(e, r, w)
        ep, v = divmod(s.cnt[e], EPOCH)
        s.cnt[e] += 1
        key = ('c', e, ep)
        s.keys.add(key)
        tok = (key, v + 1, e, 'c')
        s.ops[e].append((waits, fn, key, 1))
        s._upd(tok, r, w)

    def dma(s, e, fn, r=(), w=()):
        waits = s._deps(e, r, w)
        sl = s.slot[e]
        s.slot[e] = (sl + 1) % NSLOT[e]
        key = ('d', e, sl)
        s.keys.add(key)
        prev = s.dcnt.get(key, 0)
        if prev > 0 and s.waited[e].get(key, 0) < prev:
            s.waited[e][key] = prev
            waits.append((key, prev))
        s.dcnt[key] = prev + 16
        tok = (key, s.dcnt[key], e, 'd')
        s.ops[e].append((waits, fn, key, 16))
        s._upd(tok, r, w)

    def barrier(s):
        toks = list(s.last.values())
        for e in s.ops:
            waits = []
            for (key, val, te, kind) in toks:
                if kind == 'c' and te == e:
                    continue
                if s.waited[e].get(key, 0) >= val:
                    continue
                s.waited[e][key] = val
                waits.append((key, val))
            if waits:
                s.ops[e].append((waits, None, None, 0))

    def emit(s, es):
        nc = s.nc
        sems = {}
        for i, key in enumerate(sorted(s.keys)):
            sems[key] = es.enter_context(nc.semaphore("s%d" % i))
        with nc.Block() as block:
            def run(e):
                def f(eng):
                    for waits, fn, key, inc in s.ops[e]:
                        for (k, v) in waits:
                            eng.wait_ge(sems[k], v)
                        if fn is not None:
                            fn(eng).then_inc(sems[key], inc)
                return f
            block.sync(run('sp'))
            block.tensor(run('pe'))
            block.scalar(run('act'))
            block.vector(run('dve'))
            block.gpsimd(run('pool'))


def _bufs(*vs):
    out = []
    for v in vs:
        if isinstance(v, V):
            for b in v.bufs:
                if b not in out:
                    out.append(b)
    return out


def _ap(x):
    return x.ap if isinstance(x, V) else x


class K:
    def __init__(s, S):
        s.S = S

    def mm(s, out, lhsT, rhs, start=True, stop=True):
        s.S.op('pe', lambda e: e.matmul(out.ap, lhsT.ap, rhs.ap, start=start, stop=stop),
               _bufs(lhsT, rhs), _bufs(out))

    def tr(s, out, in_, ident):
        s.S.op('pe', lambda e: e.transpose(out.ap, in_.ap, ident.ap), _bufs(in_, ident), _bufs(out))

    def act(s, out, in_, func, bias=0.0, scale=1.0):
        s.S.op('act', lambda e: e.activation(out=out.ap, in_=in_.ap, func=func, bias=_ap(bias), scale=_ap(scale)),
               _bufs(in_, bias, scale), _bufs(out))

    def tt(s, eng, out, a, b, op):
        s.S.op(eng, lambda e: e.tensor_tensor(out.ap, a.ap, b.ap, op), _bufs(a, b), _bufs(out))

    def ts(s, eng, out, a, s1, s2, op0, op1=None):
        if op1 is None:
            s.S.op(eng, lambda e: e.tensor_scalar(out.ap, a.ap, _ap(s1), None, op0), _bufs(a, s1), _bufs(out))
        else:
            s.S.op(eng, lambda e: e.tensor_scalar(out.ap, a.ap, _ap(s1), _ap(s2), op0, op1),
                   _bufs(a, s1, s2), _bufs(out))

    def stt(s, eng, out, a, sc, b, op0, op1):
        eng = 'dve'
        s.S.op(eng, lambda e: e.scalar_tensor_tensor(out.ap, a.ap, _ap(sc), b.ap, op0, op1),
               _bufs(a, sc, b), _bufs(out))

    def cp(s, eng, out, in_):
        if eng == 'act':
            s.S.op('act', lambda e: e.copy(out.ap, in_.ap), _bufs(in_), _bufs(out))
        else:
            s.S.op(eng, lambda e: e.tensor_copy(out.ap, in_.ap), _bufs(in_), _bufs(out))

    def rcp(s, out, in_):
        s.S.op('dve', lambda e: e.reciprocal(out.ap, in_.ap), _bufs(in_), _bufs(out))

    def ms(s, eng, out, val):
        s.S.op(eng, lambda e: e.memset(out.ap, val), (), _bufs(out))

    def scan(s, out, d0, d1, init):
        s.S.op('dve', lambda e: e.tensor_tensor_scan(out.ap, d0.ap, d1.ap, _ap(init), ALU.mult, ALU.add),
               _bufs(d0, d1, init), _bufs(out))

    def ld(s, out, in_, eng='sp'):
        s.S.dma(eng, lambda e: e.dma_start(out=out.ap, in_=in_.ap), _bufs(in_), _bufs(out))


def build(nc, T_S, debug=False):
    TT = 512
    Ttot = T_S + 512
    NT = Ttot // TT
    NTS = T_S // TT
    NCS = T_S // 128
    es = ExitStack()
    S = Sched(nc)
    k = K(S)

    def din(name, shape):
        return Buf(nc.dram_tensor(name, list(shape), F32, kind="ExternalInput").ap(), name)

    def dout(name, shape):
        return Buf(nc.dram_tensor(name, list(shape), F32, kind="ExternalOutput").ap(), name)

    def dscr(name, shape):
        return Buf(nc.dram_tensor(name, list(shape), F32, kind="ExternalOutput" if debug else "Internal").ap(), name)

    x_all = din("x_all", [Ttot, D])
    condT = din("condT", [128, 16, 2])
    s0T = din("s0T", [L, 2, 8, 128, 64])
    h0 = din("h0", [L, 2, 128, 8])
    vecs = din("vecs", [128, NVEC])
    consts = din("consts", [128, 8, 512])
    ada_w = din("ada_w", [L, D, 3 * D])
    w_in = din("w_in", [L, D, NIN])
    w_out_a = din("w_out_a", [L, 1024, D])
    w_out_b = din("w_out_b", [L, 1024, D])
    w_out = din("w_out", [L, D, D])
    dec_w1 = din("dec_w1", [L, 2, D, 96])
    dec_w2 = din("dec_w2", [L, 2, 96, 1024])
    iclr_w1 = din("iclr_w1", [L, 2, D, 96])
    iclr_w2 = din("iclr_w2", [L, 2, 96, 1024])
    vres_w1 = din("vres_w1", [1, D, 64])
    vres_w2 = din("vres_w2", [1, 64, 1024])
    gr_w = din("gr_w", [L, 2, 16, 64, 64])
    gi_w = din("gi_w", [L, 2, 16, 64, 64])

    y_all = dout("y_all", [Ttot, D])
    ns_rwkv = dout("ns_rwkv", [2, L, 2, 128, 8, 64])
    ns_lru = dout("ns_lru", [2, L, 2, 128, 8])

    XT = [dscr("XT%d" % i, [D, Ttot]) for i in range(3)]
    Z = dscr("Z", [NIN, Ttot])
    LOR = dscr("LOR", [5 * 1024, Ttot])
    Z2 = dscr("Z2", [4 * 1024, Ttot])
    VF = dscr("VF", [1024, Ttot])
    YY = dscr("YY", [1024, Ttot])
    BO = dscr("BO", [1024, Ttot])
    HF = dscr("HF", [1024, Ttot])
    HB = dscr("HB", [1024, Ttot])

    uid = [0]

    def sb(st, name, shape, dt=F32):
        uid[0] += 1
        name = "%s_%d" % (name, uid[0])
        return Buf(st.enter_context(nc.sbuf_tensor(name, list(shape), dt)), name)

    def pt(st, name, cols):
        uid[0] += 1
        name = "%s_%d" % (name, uid[0])
        return Buf(st.enter_context(nc.psum_tensor(name, [128, cols], F32)), name, excl=True)

    VEC = sb(es, "VEC", [128, NVEC])
    CST = sb(es, "CST", [128, 8, 512])
    ONESR = sb(es, "ONESR", [128, 128], F32R)
    BONE = sb(es, "BONE", [128, 128])
    BONER = sb(es, "BONER", [128, 128], F32R)
    MOD = sb(es, "MOD", [128, 48, 2])
    GS = sb(es, "GS", [128, 16, 2])
    ONE1 = sb(es, "ONE1", [128, 128])
    k.ld(VEC[:, :], vecs[:, :])
    k.ld(CST[:, :, :], consts[:, :, :])
    ZER = sb(es, "ZER", [128, 128])
    k.ms('pool', ZER[:, :], 0.0)
    k.ms('pool', ONE1[:, 0:128], 1.0 / D)
    k.cp('dve', ONESR[:, :], ONE1[:, 0:128])
    k.ms('pool', ONE1[:, :], 1.0)
    k.cp('dve', BONE[:, :], CST[:, 1, 0:128])
    k.cp('dve', BONER[:, :], CST[:, 1, 0:128])
    IDENT = CST[:, 0, 0:128]

    def vcol(name, l, j=0, n=1):
        o = VOFF[(name, l)] + j
        return VEC[:, o:o + n]

    tiles = []
    for i in range(NTS):
        tiles.append((i * TT, 0))
    tiles.append((T_S, 1))
    seqs = [(0, T_S, 'grid'), (T_S, 256, 'p0'), (T_S + 256, 256, 'p1')]

    def featv(dbuf, row0, nrows, t0, n):
        return dbuf[row0:row0 + nrows, t0:t0 + n].re("(c p) t -> p c t", p=128)

    with ExitStack() as st:
        XTOK = [sb(st, "XTOK%d" % i, [128, 4, D]) for i in range(2)]
        XTB = [sb(st, "XTB%d" % i, [128, 16, TT]) for i in range(2)]
        PSs = [pt(st, "P0ps%d" % i, 512) for i in range(4)]
        for ti, (t0, ci) in enumerate(tiles):
            xt = XTOK[ti % 2]
            xb = XTB[ti % 2]
            k.ld(xt[:, :, :], x_all[t0:t0 + TT, :].re("(s p) d -> p s d", p=128))
            for dc in range(16):
                ps = PSs[dc % 4]
                for s_ in range(4):
                    k.tr(ps[:, s_ * 128:(s_ + 1) * 128], xt[:, s_, dc * 128:(dc + 1) * 128], IDENT)
                k.cp('act' if dc % 2 else 'dve', xb[:, dc, :], ps[:, :])
            k.ld(featv(XT[0], 0, D, t0, TT), xb[:, :, :])
    S.barrier()

    def rms_to(st_, xb, PSS, RSTD, SQ):
        for dc in range(16):
            sq = SQ[dc % 2]
            k.act(sq[:, :], xb[:, dc, :], AF.Square)
            k.mm(PSS[:, :], ONESR[:, :], sq[:, :], start=(dc == 0), stop=(dc == 15))
        k.act(RSTD[:, :], PSS[:, :], AF.Ln, bias=vcol('eps_rms', 0), scale=1.0)
        k.act(RSTD[:, :], RSTD[:, :], AF.Exp, scale=-0.5)

    for l in range(L):
        XIN = XT[l]
        XOUT = XT[l + 1]
        with ExitStack() as st:
            SC = sb(st, "SC", [128, 16, 2])
            AW = [sb(st, "AW%d" % i, [128, 16, 512]) for i in range(2)]
            PA = pt(st, "PA", 512)
            k.ld(SC[:, :, :], condT[:, :, :])
            k.act(SC[:, :, :], SC[:, :, :], AF.Silu)
            for wt in range(12):
                aw = AW[wt % 2]
                k.ld(aw[:, :, :], ada_w[l, :, wt * 512:(wt + 1) * 512].re("(k p) n -> p k n", p=128))
                for j in range(4):
                    oc = wt * 4 + j
                    for kc in range(16):
                        k.mm(PA[:, oc * 2:oc * 2 + 2], aw[:, kc, j * 128:(j + 1) * 128], SC[:, kc, :],
                             start=(kc == 0), stop=(kc == 15))
            for c in range(2):
                k.tt('dve', MOD[:, :, c], PA[:, 0:96].re("p (o c) -> p o c", c=2)[:, :, c], vcol('ada_b', l, 0, 48), ALU.add)
            for c in range(2):
                k.stt('dve', GS[:, :, c], MOD[:, 16:32, c], 1.0, vcol('norm_g', l, 0, 16), ALU.add, ALU.mult)
        S.barrier()

        with ExitStack() as st:
            GW = 2 * TT
            XB = [sb(st, "XB%d" % i, [128, 16, TT]) for i in range(2)]
            XM = sb(st, "XM", [128, 16, GW], MMDT)
            WB = [sb(st, "WB%d" % i, [128, 16, 256], MMDT) for i in range(2)]
            ZST = [sb(st, "ZST%d" % i, [128, 2, GW]) for i in range(2)]
            LST = sb(st, "LST", [128, 8, TT])
            LW1 = [sb(st, "LW1", [128, 16, 96], MMDT)] * 2
            LW2 = [sb(st, "LW2", [96, 1024], F32R)] * 2
            T1 = sb(st, "T1", [96, TT], F32R)
            SQ = [sb(st, "SQ%d" % i, [128, TT], F32R) for i in range(2)]
            RSTD = sb(st, "RSTD", [128, TT])
            TMP = [sb(st, "TMP%d" % i, [128, TT]) for i in range(2)]
            PSS = pt(st, "PSS", 512)
            PZ = [pt(st, "PZ%d" % i, 512) for i in range(4)]
            PL = [pt(st, "PL%d" % i, 512) for i in range(3)]
            loras = [('dec', 0, dec_w1, dec_w2, 96, 'dec_w0_0'), ('dec', 1, dec_w1, dec_w2, 96, 'dec_w0_1'),
                     ('iclr', 0, iclr_w1, iclr_w2, 96, 'iclr_w0_0'), ('iclr', 1, iclr_w1, iclr_w2, 96, 'iclr_w0_1')]
            if l > 0:
                loras.append(('vres', 0, vres_w1, vres_w2, 64, 'vres_w0'))
            groups = []
            ti_ = 0
            while ti_ < len(tiles):
                t0_, ci_ = tiles[ti_]
                if ci_ == 0 and ti_ + 1 < len(tiles) and tiles[ti_ + 1][1] == 0:
                    groups.append((t0_, 2, ci_))
                    ti_ += 2
                else:
                    groups.append((t0_, 1, ci_))
                    ti_ += 1
            nl = 0
            nxb = 0
            npz = 0
            for (g0, nh, ci) in groups:
                for hf in range(nh):
                    xb = XB[nxb % 2]
                    nxb += 1
                    k.ld(xb[:, :, :], featv(XIN, 0, D, g0 + hf * TT, TT))
                    rms_to(st, xb, PSS, RSTD, SQ)
                    for dc in range(16):
                        tmp = TMP[dc % 2]
                        k.stt('dve', tmp[:, :], xb[:, dc, :], GS[:, dc, ci:ci + 1], RSTD[:, :], ALU.mult, ALU.mult)
                        k.act(XM[:, dc, hf * TT:(hf + 1) * TT], tmp[:, :], AF.Identity, bias=MOD[:, dc, ci:ci + 1], scale=1.0)
                for wt in range(NIN // 256):
                    wb = WB[wt % 2]
                    k.ld(wb[:, :, :], w_in[l, :, wt * 256:(wt + 1) * 256].re("(k p) n -> p k n", p=128), eng='pool')
                    zs = ZST[wt % 2]
                    for j in range(2):
                        for hf in range(nh):
                            ps = PZ[npz % 4]
                            npz += 1
                            for kc in range(16):
                                k.mm(ps[:, :], wb[:, kc, j * 128:(j + 1) * 128], XM[:, kc, hf * TT:(hf + 1) * TT],
                                     start=(kc == 0), stop=(kc == 15))
                            k.cp('act' if (j + hf) % 2 else 'dve', zs[:, j, hf * TT:(hf + 1) * TT], ps[:, :])
                    k.ld(featv(Z, wt * 256, 256, g0, nh * TT), zs[:, :, 0:nh * TT])
                for li, (nm, d_, w1, w2, R, w0n) in enumerate(loras):
                    lw1 = LW1[nl % 2]
                    lw2 = LW2[nl % 2]
                    nl += 1
                    if nm == 'vres':
                        k.ld(lw1[:, :, 0:R], w1[0, :, :].re("(k p) r -> p k r", p=128), eng='pool')
                        k.ld(lw2[0:R, :], w2[0, :, :], eng='pool')
                    else:
                        k.ld(lw1[:, :, 0:R], w1[l, d_, :, :].re("(k p) r -> p k r", p=128), eng='pool')
                        k.ld(lw2[0:R, :], w2[l, d_, :, :], eng='pool')
                    for hf in range(nh):
                        p1 = PL[2]
                        for kc in range(16):
                            k.mm(p1[0:R, :], lw1[:, kc, 0:R], XM[:, kc, hf * TT:(hf + 1) * TT], start=(kc == 0), stop=(kc == 15))
                        if nm == 'dec':
                            k.act(T1[0:R, :], p1[0:R, :], AF.Tanh)
                        else:
                            k.cp('dve', T1[0:R, :], p1[0:R, :])
                        for oc in range(8):
                            p2 = PL[oc % 2]
                            k.mm(p2[:, :], lw2[0:R, oc * 128:(oc + 1) * 128], T1[0:R, :])
                            k.act(LST[:, oc, :], p2[:, :], AF.Sigmoid, bias=vcol(w0n, l if nm != 'vres' else 1, oc), scale=1.0)
                        k.ld(featv(LOR, li * 1024, 1024, g0 + hf * TT, TT), LST[:, :, :])
        S.barrier()

        with ExitStack() as st:
            BIN = [sb(st, "BIN%d" % i, [128, 8, 640]) for i in range(2)]
            ACCs = [sb(st, "ACC%d" % i, [128, 8, TT]) for i in range(2)]
            MIX = [sb(st, "MIX%d" % i, [128, 8, TT]) for i in range(3)]
            VFt = sb(st, "VFt", [128, 8, TT])
            VGt = sb(st, "VGt", [128, 8, TT])
            KK = sb(st, "KK", [128, 8, TT])
            SQK = [sb(st, "SQK%d" % i, [128, TT], F32R) for i in range(2)]
            RN = [sb(st, "RN%d" % i, [128, TT]) for i in range(2)]
            KAP = KK
            PK = [pt(st, "PK%d" % i, 512) for i in range(2)]
            nb = 0
            for ti, (t0, ci) in enumerate(tiles):
                for q, mun in enumerate(('mu_r', 'mu_k', 'mu_v')):
                    B = BIN[nb % 2]
                    ACC = ACCs[nb % 2]
                    nb += 1
                    mix = MIX[q]
                    if ci == 0:
                        lo = t0 - 64 if t0 > 0 else t0
                        hi = t0 + TT + 64 if t0 + TT < T_S else t0 + TT
                        if t0 == 0:
                            k.ms('pool', B[:, :, 0:64], 0.0)
                        if t0 + TT >= T_S:
                            k.ms('pool', B[:, :, 576:640], 0.0)
                        k.ld(B[:, :, 64 - (t0 - lo):576 + (hi - t0 - TT)], featv(Z, q * 1024, 1024, lo, hi - lo))
                        C = B[:, :, 64:576]
                        k.tt('pool', ACC[:, 0:3, :], B[:, 0:3, 0:512], B[:, 0:3, 128:640], ALU.add)
                        k.tt('dve', ACC[:, 3:8, :], B[:, 3:8, 0:512], B[:, 3:8, 128:640], ALU.add)
                        C4 = C.re("p c (r w) -> p c r w", w=64)
                        A4 = ACC[:, :, :].re("p c (r w) -> p c r w", w=64)
                        k.tt('dve', A4[:, :, :, 1:64], A4[:, :, :, 1:64], C4[:, :, :, 0:63], ALU.add)
                        k.tt('dve', A4[:, :, :, 0:63], A4[:, :, :, 0:63], C4[:, :, :, 1:64], ALU.add)
                        sc = 0.25
                    else:
                        k.ld(B[:, :, 64:576], featv(Z, q * 1024, 1024, t0, TT))
                        C = B[:, :, 64:576]
                        C4 = C.re("p c (s w) -> p c s w", w=256)
                        A4 = ACC[:, :, :].re("p c (s w) -> p c s w", w=256)
                        k.ms('pool', A4[:, :, :, 0:1], 0.0)
                        k.cp('pool', A4[:, :, :, 1:256], C4[:, :, :, 0:255])
                        k.tt('dve', A4[:, :, :, 0:255], A4[:, :, :, 0:255], C4[:, :, :, 1:256], ALU.add)
                        sc = 0.5
                    k.stt('dve', ACC[:, :, :], ACC[:, :, :], sc, C, ALU.mult, ALU.subtract)
                    for fc in range(8):
                        k.stt('pool' if fc % 2 else 'dve', mix[:, fc, :], ACC[:, fc, :], vcol(mun, l, fc), C[:, fc, :], ALU.mult, ALU.add)
                    if q == 0:
                        k.ld(featv(Z2, 0, 1024, t0, TT), mix[:, :, :])
                    elif q == 1:
                        k.ld(featv(Z2, 1024, 1024, t0, TT), mix[:, :, :])
                        for fc in range(8):
                            k.act(KK[:, fc, :], mix[:, fc, :], AF.Identity, bias=0.0, scale=vcol('k_k', l, fc))
                        for fc in range(8):
                            sq = SQK[fc % 2]
                            rn = RN[fc % 2]
                            k.act(sq[:, :], KK[:, fc, :], AF.Square)
                            k.mm(PK[fc % 2][:, :], BONER[:, :], sq[:, :])
                            k.ts('dve', rn[:, :], PK[fc % 2][:, :], 1e-24, None, ALU.max)
                            k.act(rn[:, :], rn[:, :], AF.Ln)
                            k.act(rn[:, :], rn[:, :], AF.Exp, scale=-0.5)
                            k.tt('dve', KAP[:, fc, :], KK[:, fc, :], rn[:, :], ALU.mult)
                        k.ld(featv(Z2, 3072, 1024, t0, TT), KAP[:, :, :])
                    else:
                        if l > 0:
                            k.ld(VFt[:, :, :], featv(VF, 0, 1024, t0, TT))
                            k.ld(VGt[:, :, :], featv(LOR, 4096, 1024, t0, TT))
                            k.tt('pool', VFt[:, :, :], VFt[:, :, :], mix[:, :, :], ALU.subtract)
                            k.tt('dve', VFt[:, :, :], VFt[:, :, :], VGt[:, :, :], ALU.mult)
                            k.tt('dve', mix[:, :, :], mix[:, :, :], VFt[:, :, :], ALU.add)
                        else:
                            k.ld(featv(VF, 0, 1024, t0, TT), mix[:, :, :])
                        k.ld(featv(Z2, 2048, 1024, t0, TT), mix[:, :, :])
        S.barrier()

        with ExitStack() as st:
            def t4(name, dt=F32):
                return sb(st, name, [128, 8, 128], dt)
            LD = [[t4("LD%d_%d" % (i, j)) for j in range(6)] for i in range(2)]
            CL, CLN, EXPB, TMP2, BON = t4("CL"), t4("CLN"), t4("EXPB"), t4("TMP2"), t4("BON")
            QR = sb(st, "QR", [128, 8, 2, 128], F32R)
            KH, BH = t4("KH", F32R), t4("BH", F32R)
            KAb = vcol('k_a', l, 0, 8).bcast(2, [128, 8, 128])
            RKb = vcol('r_k', l, 0, 8).bcast(2, [128, 8, 128])
            VT_, KT_, BT_ = (sb(st, n, [128, 1024]) for n in ("VT_", "KT_", "BT_"))
            AKK, ARK, ARB = (sb(st, n, [128, 16, 128]) for n in ("AKK", "ARK", "ARB"))
            XN = sb(st, "XN", [128, 16, 128], F32R)
            XTPT = sb(st, "XTPT", [128, 16, 2, 128], F32R)
            XNq = [Buf(XN.t, "XNq%d" % q) for q in range(4)]
            XTPTq = [Buf(XTPT.t, "XTPTq%d" % q) for q in range(4)]
            RH, NU = sb(st, "RH", [128, 1024]), sb(st, "NU", [128, 1024])
            ST_, S0P = sb(st, "ST_", [128, 8, 64]), sb(st, "S0P", [128, 8, 64])
            YT = [t4("YT0")] * 2
            YP = [t4("YP0")] * 2
            BP = [t4("BP0")] * 2
            EI, EO = sb(st, "EI", [128, 8]), sb(st, "EO", [128, 8])
            PQa = [pt(st, "PQa%d" % i, 1024) for i in range(2)]
            PQb = [pt(st, "PQb%d" % i, 512) for i in range(2)]
            PQa3 = PQa + [pt(st, "PQa2", 1024)]
            ninv = [0]
            RSTM = [sb(st, "RSTM%d" % i, [128, 1024]) for i in range(2)]
            for i_ in range(2):
                k.ms('pool', RSTM[i_][:, :], 1.0)
                z_ = 0 if i_ == 0 else 127
                k.ms('pool', RSTM[i_][:, :].re("p (c t) -> p c t", t=128)[:, :, z_:z_ + 1], 0.0)

            def m4(typ):
                return CST[:, typ, :].re("p (h t) -> p h t", h=4)
            M_GT, M_GE, M_LT, M_LE, M_NGT, M_NLT = 2, 3, 4, 5, 6, 7
            nld = 0
            flat = []
            for d__ in range(2):
                for (sq0_, sqn_, kind_) in seqs:
                    n_ = sqn_ // 128
                    for ci_ in (range(n_) if d__ == 0 else range(n_ - 1, -1, -1)):
                        flat.append((d__, sq0_ + ci_ * 128))

            def issue_loads(i):
                dd, cc = flat[i]
                tl_ = LD[i % 2]
                for j in range(4):
                    k.ld(tl_[j][:, :, :], featv(Z2, j * 1024, 1024, cc, 128))
                k.ld(tl_[4][:, :, :], featv(LOR, (2 + dd) * 1024, 1024, cc, 128))
                k.ld(tl_[5][:, :, :], featv(LOR, dd * 1024, 1024, cc, 128))

            EI2 = [sb(st, "EI2%d" % i, [128, 8]) for i in range(2)]

            def prepA_pool1(i):
                r_, k_, v_, kp_, a_, sg_ = LD[i % 2]
                k.ts('pool', TMP2[:, :, :], a_[:, :, :], -1.0, None, ALU.add)
                k.tt('pool', TMP2[:, :, :], TMP2[:, :, :], KAb, ALU.mult)

            def prepA_dve(i):
                dd, cc = flat[i]
                r_, k_, v_, kp_, a_, sg_ = LD[i % 2]
                md = 63 if dd == 0 else 64
                k.stt('dve', k_[:, :, :], TMP2[:, :, :], 1.0, k_[:, :, :], ALU.add, ALU.mult)
                k.ts('dve', sg_[:, :, :], sg_[:, :, :], -float(np.exp(-0.5)), None, ALU.mult)
                clf = CL[:, :, :].re("p c t -> p (c t)")
                sgf = sg_[:, :, :].re("p c t -> p (c t)")
                if dd == 0:
                    k.scan(clf, RSTM[0][:, :], sgf, 0.0)
                else:
                    k.scan(clf[:, ::-1], RSTM[1][:, ::-1], sgf[:, ::-1], 0.0)
                k.tt('dve', CLN[:, :, :], CL[:, :, :], CL[:, :, md:md + 1].bto([128, 8, 128]), ALU.subtract)
                k.act(EI2[i % 2][:, :], CL[:, :, md], AF.Exp)

            def prepA_pool2(i):
                r_, k_, v_, kp_, a_, sg_ = LD[i % 2]
                k.tt('pool', a_[:, :, :], kp_[:, :, :], a_[:, :, :], ALU.mult)
                k.tt('pool', TMP2[:, :, :], r_[:, :, :], k_[:, :, :], ALU.mult)
                k.tt('pool', TMP2[:, :, :], TMP2[:, :, :], RKb, ALU.mult)
            for d_ in range(2):
                if d_ == 0:
                    m_strT, m_incT, m_nstrT, m_nstrN = M_GT, M_GE, M_NGT, M_NLT
                    mid, last = 63, 127
                else:
                    m_strT, m_incT, m_nstrT, m_nstrN = M_LT, M_LE, M_NLT, M_NGT
                    mid, last = 64, 0
                for si, (sq0, sqn, kind) in enumerate(seqs):
                    nchs = sqn // 128
                    if kind == 'grid':
                        k.ld(ST_[:, :, :], s0T[l, d_, :, :, :].re("c p v -> p c v"))
                    else:
                        k.ms('pool', ST_[:, :, :], 0.0)
                    order = range(nchs) if d_ == 0 else range(nchs - 1, -1, -1)
                    for cidx in order:
                        c0 = sq0 + cidx * 128
                        R_, K_, V_, KP_, A_, SG_ = LD[nld % 2]
                        YTt, YPt, BPt = YT[nld % 2], YP[nld % 2], BP[nld % 2]
                        nld += 1
                        if nld == 1:
                            issue_loads(0)
                        if nld < len(flat):
                            issue_loads(nld)
                        if d_ == 1:
                            k.ld(YPt[:, :, :], featv(YY, 0, 1024, c0, 128))
                            k.ld(BPt[:, :, :], featv(BO, 0, 1024, c0, 128))
                        if nld == 1:
                            prepA_pool1(0)
                            prepA_dve(0)
                            prepA_pool2(0)
                        EI = EI2[(nld - 1) % 2]
                        pb = PQa[0]
                        for fc in range(8):
                            k.mm(pb[:, fc * 128:(fc + 1) * 128], BONE[:, :], TMP2[:, fc, :])
                        if d_ == 0:
                            k.tt('dve', BON[:, :, :], pb[:, :].re("p (c t) -> p c t", t=128), V_[:, :, :], ALU.mult)
                        else:
                            k.tt('dve', BON[:, :, :], pb[:, :].re("p (c t) -> p c t", t=128), V_[:, :, :], ALU.mult)
                            k.tt('pool', BON[:, :, :], BON[:, :, :], BPt[:, :, :], ALU.add)
                        k.ld(featv(BO, 0, 1024, c0, 128), BON[:, :, :])
                        k.act(EXPB[:, :, :], CLN[:, :, :], AF.Exp)
                        k.tt('dve', QR[:, :, 1, :], R_[:, :, :], EXPB[:, :, :], ALU.mult)
                        k.cp('act', EO[:, :], EXPB[:, :, last])
                        k.tt('pool', TMP2[:, :, :], CLN[:, :, :], SG_[:, :, :], ALU.subtract)
                        k.act(CL[:, :, :], TMP2[:, :, :], AF.Exp)
                        k.tt('dve', QR[:, :, 0, :], KP_[:, :, :], CL[:, :, :], ALU.mult)
                        k.act(EXPB[:, :, :], CLN[:, :, :], AF.Exp, scale=-1.0)
                        k.tt('dve', KH[:, :, :], K_[:, :, :], EXPB[:, :, :], ALU.mult)
                        k.tt('pool', BH[:, :, :], A_[:, :, :], EXPB[:, :, :], ALU.mult)
                        for (src, dst, pq) in ((V_, VT_, PQa[1]), (KH, KT_, PQa[0]), (BH, BT_, PQa[1])):
                            for fc in range(8):
                                sv = src[:, fc, :]
                                if src is not V_:
                                    sv = sv.bc(F32)
                                k.tr(pq[:, fc * 128:(fc + 1) * 128], sv, IDENT)
                            k.cp('act' if dst is KT_ else 'dve', dst[:, :], pq[:, :])
                        for q in range(4):
                            for hq in range(4):
                                h = 4 * q + hq
                                fc, p0 = h // 2, (h % 2) * 64
                                qr = QR[p0:p0 + 64, fc, :, :].re("p a t -> p (a t)")
                                k.mm(PQa[0][:, hq * 256:(hq + 1) * 256], BH[p0:p0 + 64, fc, :], qr)
                                k.mm(PQa[1][:, hq * 256:(hq + 1) * 256], KH[p0:p0 + 64, fc, :], qr)
                                k.mm(PQb[0][:, hq * 128:(hq + 1) * 128], QR[p0:p0 + 64, fc, 0, :], BH[p0:p0 + 64, fc, :])
                            hs = slice(4 * q, 4 * q + 4)
                            a1 = PQa[0][:, :].re("p (h a t) -> p h a t", h=4, a=2)
                            a2 = PQa[1][:, :].re("p (h a t) -> p h a t", h=4, a=2)
                            a3 = PQb[0][:, :].re("p (h t) -> p h t", h=4)
                            k.tt('dve', XTPTq[q][:, hs, 0, :], a1[:, :, 0, :], m4(m_nstrT), ALU.mult)
                            k.tt('dve', ARB[:, hs, :], a1[:, :, 1, :], m4(m_incT), ALU.mult)
                            k.tt('dve', AKK[:, hs, :], a2[:, :, 0, :], m4(m_strT), ALU.mult)
                            k.tt('dve', ARK[:, hs, :], a2[:, :, 1, :], m4(m_incT), ALU.mult)
                            k.tt('dve', XNq[q][:, hs, :], a3[:, :, :], m4(m_nstrN), ALU.mult)
                            k.cp('pool', XTPTq[q][:, hs, 1, :], CST[:, 0, :].re("p (h t) -> p h t", h=4))
                        for step in range(7):
                            lastst = (step == 6)
                            for q in range(4):
                                pa, pbk = PQa3[ninv[0] % 3], PQb[ninv[0] % 2]
                                ninv[0] += 1
                                hs = slice(4 * q, 4 * q + 4)
                                for hq in range(4):
                                    h = 4 * q + hq
                                    if not lastst:
                                        k.mm(pa[:, hq * 256:(hq + 1) * 256], XNq[q][:, h, :],
                                             XTPTq[q][:, h, :, :].re("p a t -> p (a t)"))
                                        k.mm(pbk[:, hq * 128:(hq + 1) * 128], XTPTq[q][:, h, 0, :], XNq[q][:, h, :])
                                    else:
                                        k.mm(pa[:, hq * 256 + 128:(hq + 1) * 256], XNq[q][:, h, :], XTPTq[q][:, h, 1, :])
                                a1 = pa[:, :].re("p (h a t) -> p h a t", h=4, a=2)
                                k.tt('dve', XTPTq[q][:, hs, 1, :], a1[:, :, 1, :], XTPTq[q][:, hs, 1, :], ALU.add)
                                if not lastst:
                                    k.cp('dve', XTPTq[q][:, hs, 0, :], a1[:, :, 0, :])
                                    k.cp('act', XNq[q][:, hs, :], pbk[:, :].re("p (h t) -> p h t", h=4))
                            if nld < len(flat):
                                if step == 1:
                                    prepA_pool1(nld)
                                elif step == 3:
                                    prepA_dve(nld)
                                elif step == 5:
                                    prepA_pool2(nld)
                        k.tt('dve', S0P[:, :, :], ST_[:, :, :], EI[:, :].bcast(2, [128, 8, 64]), ALU.mult)
                        pr = PQa[0]
                        for h in range(16):
                            fc, p0 = h // 2, (h % 2) * 64
                            k.mm(pr[:, h * 64:(h + 1) * 64], QR[p0:p0 + 64, fc, 0, :].bc(F32), S0P[p0:p0 + 64, fc, :], start=True, stop=False)
                            k.mm(pr[:, h * 64:(h + 1) * 64], AKK[:, h, :], VT_[:, h * 64:(h + 1) * 64], start=False, stop=True)
                        k.cp('act', RH[:, :], pr[:, :])
                        pu = PQa[1]
                        for h in range(16):
                            k.mm(pu[:, h * 64:(h + 1) * 64], XTPTq[h // 4][:, h, 1, :].bc(F32), RH[:, h * 64:(h + 1) * 64])
                        k.ts('dve', NU[:, :], pu[:, :], -1.0, None, ALU.mult)
                        py = PQa[0]
                        for h in range(16):
                            fc, p0 = h // 2, (h % 2) * 64
                            o = py[p0:p0 + 64, fc * 128:(fc + 1) * 128]
                            k.mm(o, S0P[p0:p0 + 64, fc, :], QR[p0:p0 + 64, fc, 1, :].bc(F32), start=True, stop=False)
                            k.mm(o, VT_[:, h * 64:(h + 1) * 64], ARK[:, h, :], start=False, stop=False)
                            k.mm(o, NU[:, h * 64:(h + 1) * 64], ARB[:, h, :], start=False, stop=True)
                        pyv = py[:, :].re("p (c t) -> p c t", t=128)
                        if d_ == 0:
                            k.cp('act', YTt[:, :, :], pyv)
                        else:
                            k.tt('dve', YTt[:, :, :], pyv, YPt[:, :, :], ALU.add)
                        k.ld(featv(YY, 0, 1024, c0, 128), YTt[:, :, :])
                        pss = PQa[1]
                        for fc in range(8):
                            cs = slice(fc * 128, (fc + 1) * 128)
                            k.mm(pss[:, cs], KT_[:, cs], VT_[:, cs], start=True, stop=False)
                            k.mm(pss[:, cs], BT_[:, cs], NU[:, cs], start=False, stop=True)
                        psv = pss[:, :].re("p (c t) -> p c t", t=128)
                        for hp in range(2):
                            blk = slice(hp * 64, (hp + 1) * 64)
                            k.tt('dve', ST_[blk, :, :], psv[blk, :, hp * 64:(hp + 1) * 64], S0P[blk, :, :], ALU.add)
                        k.tt('dve', ST_[:, :, :], ST_[:, :, :], EO[:, :].bcast(2, [128, 8, 64]), ALU.mult)
                    if kind != 'grid':
                        pi = 0 if kind == 'p0' else 1
                        k.ld(ns_rwkv[pi, l, d_, :, :, :], ST_[:, :, :])
        S.barrier()

        with ExitStack() as st:
            XBn = [sb(st, "XBn%d" % i, [128, 8, 515]) for i in range(2)]
            XC = sb(st, "XC", [128, 8, TT], F32R)
            BD = [[sb(st, "BD%d_%d" % (g, d_), [128, 8, 128], F32R) for d_ in range(2)] for g in range(2)]
            RG = sb(st, "RG", [128, 8, TT])
            IG = sb(st, "IG", [128, 8, TT])
            AA = sb(st, "AA", [128, 8, TT])
            UU = sb(st, "UU", [128, 8, TT])
            HH = [sb(st, "HH%d" % i, [128, 8, TT]) for i in range(2)]
            HC = sb(st, "HC", [128, 8])
            CLM = sb(st, "CLM", [128, 16])
            PG = [pt(st, "PG%d" % i, 512) for i in range(4)]
            for g, gw in enumerate((gr_w, gi_w)):
                for d_ in range(2):
                    k.cp('pool', BD[g][d_][:, :, :], ZER[:, :].bcast(1, [128, 8, 128]))
                    for hp in range(2):
                        k.ld(BD[g][d_][hp * 64:(hp + 1) * 64, :, hp * 64:(hp + 1) * 64],
                             gw[l, d_, :, :, :].re("(c a) i j -> a i c j", a=2)[hp], eng='pool')
            k.act(CLM[:, :], vcol('lam_0', l, 0, 16), AF.Exp, scale=-1.0)
            k.act(CLM[:, :], CLM[:, :], AF.Ln, bias=1.0)
            k.ts('dve', CLM[:, :], CLM[:, :], -8.0, None, ALU.mult)
            segs = [(i * TT, TT, 0, T_S, 'grid') for i in range(NTS)] + [(T_S, 256, T_S, T_S + 256, 'p0'),
                                                                          (T_S + 256, 256, T_S + 256, T_S + 512, 'p1')]
            nseg = 0
            for d_ in range(2):
                HOUT = HF if d_ == 0 else HB
                order = segs if d_ == 0 else segs[::-1]
                for (g0, n, q0, q1, kind) in order:
                    xb = XBn[nseg % 2]
                    hh = HH[nseg % 2]
                    nseg += 1
                    lo = g0 - 2 if g0 > q0 else g0
                    hi = g0 + n + 1 if g0 + n < q1 else g0 + n
                    if g0 == q0:
                        k.ms('pool', xb[:, :, 0:2], 0.0)
                    if g0 + n >= q1:
                        k.ms('pool', xb[:, :, 2 + n:3 + n], 0.0)
                    k.ld(xb[:, :, 2 - (g0 - lo):2 + n + (hi - g0 - n)], featv(Z, 4096, 1024, lo, hi - lo))
                    for fc in range(8):
                        e_ = 'pool' if fc % 2 else 'dve'
                        k.ts(e_, UU[:, fc, 0:n], xb[:, fc, 0:n], vcol('conv_w0', l, fc), vcol('conv_b', l, fc), ALU.mult, ALU.add)
                        k.stt(e_, UU[:, fc, 0:n], xb[:, fc, 1:1 + n], vcol('conv_w1', l, fc), UU[:, fc, 0:n], ALU.mult, ALU.add)
                        k.stt(e_, UU[:, fc, 0:n], xb[:, fc, 2:2 + n], vcol('conv_w2', l, fc), UU[:, fc, 0:n], ALU.mult, ALU.add)
                        k.stt(e_, XC[:, fc, 0:n], xb[:, fc, 3:3 + n], vcol('conv_w3', l, fc), UU[:, fc, 0:n], ALU.mult, ALU.add)
                    for fc in range(8):
                        pr_, pi_ = PG[(2 * fc) % 4], PG[(2 * fc + 1) % 4]
                        k.mm(pr_[:, 0:n], BD[0][d_][:, fc, :], XC[:, fc, 0:n])
                        k.mm(pi_[:, 0:n], BD[1][d_][:, fc, :], XC[:, fc, 0:n])
                        k.act(RG[:, fc, 0:n], pr_[:, 0:n], AF.Sigmoid, bias=vcol('gr_b_%d' % d_, l, fc), scale=1.0)
                        k.act(IG[:, fc, 0:n], pi_[:, 0:n], AF.Sigmoid, bias=vcol('gi_b_%d' % d_, l, fc), scale=1.0)
                    for fc in range(8):
                        k.act(AA[:, fc, 0:n], RG[:, fc, 0:n], AF.Exp, scale=CLM[:, d_ * 8 + fc:d_ * 8 + fc + 1])
                    k.tt('pool', RG[:, :, 0:n], AA[:, :, 0:n], AA[:, :, 0:n], ALU.mult)
                    k.ts('dve', RG[:, :, 0:n], RG[:, :, 0:n], -1.0, 1.0, ALU.mult, ALU.add)
                    k.ts('dve', RG[:, :, 0:n], RG[:, :, 0:n], 1e-30, None, ALU.max)
                    k.act(RG[:, :, 0:n], RG[:, :, 0:n], AF.Ln)
                    k.act(RG[:, :, 0:n], RG[:, :, 0:n], AF.Exp, scale=0.5)
                    k.tt('pool', IG[:, :, 0:n], IG[:, :, 0:n], XC[:, :, 0:n].bc(F32), ALU.mult)
                    k.tt('dve', UU[:, :, 0:n], RG[:, :, 0:n], IG[:, :, 0:n], ALU.mult)
                    start_seq = (g0 == q0) if d_ == 0 else (g0 + n >= q1)
                    if start_seq:
                        if kind == 'grid':
                            k.ld(HC[:, :], h0[l, d_, :, :])
                        else:
                            k.ms('pool', HC[:, :], 0.0)
                    for fc in range(8):
                        if d_ == 0:
                            k.scan(hh[:, fc, 0:n], AA[:, fc, 0:n], UU[:, fc, 0:n], HC[:, fc:fc + 1])
                        else:
                            k.scan(hh[:, fc, n - 1::-1] if False else hh[:, fc, 0:n][:, ::-1], AA[:, fc, 0:n][:, ::-1], UU[:, fc, 0:n][:, ::-1], HC[:, fc:fc + 1])
                    k.cp('act', HC[:, :], hh[:, :, n - 1] if d_ == 0 else hh[:, :, 0])
                    k.ld(featv(HOUT, 0, 1024, g0, n), hh[:, :, 0:n])
                    end_seq = (g0 + n >= q1) if d_ == 0 else (g0 == q0)
                    if end_seq and kind != 'grid':
                        k.ld(ns_lru[0 if kind == 'p0' else 1, l, d_, :, :], HC[:, :])
        S.barrier()

        with ExitStack() as st:
            IN3 = [sb(st, "IN3_%d" % i, [128, 8, TT]) for i in range(3)]
            YG = sb(st, "YG", [128, 8, TT], MMDT)
            MG = sb(st, "MG", [128, 16, TT], MMDT)
            MGF = sb(st, "MGF", [128, 16, TT]) if MMDT is not F32R else None

            def mgf(dc_):
                return MGF[:, dc_, :] if MGF is not None else MG[:, dc_, :].bc(F32)
            WS = [sb(st, "WS%d" % i, [128, 4096], MMDT) for i in range(2)]
            SQ = [sb(st, "SQ4%d" % i, [128, TT], F32R) for i in range(2)]
            YC = [sb(st, "YC%d" % i, [128, TT]) for i in range(2)]
            RS = [sb(st, "RS0", [128, TT])] * 2
            MCH = [sb(st, "MCH%d" % i, [128, TT]) for i in range(2)]
            XCH = [sb(st, "XCH%d" % i, [128, TT]) for i in range(3)]
            MAH = [sb(st, "MAH%d" % i, [128, 8, TT]) for i in range(2)]
            XNS = [sb(st, "XNS%d" % i, [128, TT]) for i in range(2)]
            PP = [pt(st, "PP%d" % i, 512) for i in range(6)]
            nw = 0
            nm_ = 0
            for ti, (t0, ci) in enumerate(tiles):
                Yt, Bt, Gt = IN3

                def load_A(tt0):
                    k.ld(Yt[:, :, :], featv(YY, 0, 1024, tt0, TT))
                    k.ld(Bt[:, :, :], featv(BO, 0, 1024, tt0, TT))
                    k.ld(Gt[:, :, :], featv(Z, 3072, 1024, tt0, TT))

                def load_M(br_, hh_, tt0):
                    k.ld(MAH[hh_][:, :, :], featv(Z, 6144 + br_ * 2048 + hh_ * 1024, 1024, tt0, TT))
                if ti == 0:
                    load_A(t0)
                load_M(0, 0, t0)
                load_M(0, 1, t0)
                k.act(Gt[:, :, :], Gt[:, :, :], AF.Silu)
                for fc in range(8):
                    pm_, pv_ = PP[(2 * fc) % 4], PP[(2 * fc + 1) % 4]
                    yc, rs, sq = YC[fc % 2], RS[fc % 2], SQ[fc % 2]
                    k.mm(pm_[:, :], BONE[:, :], Yt[:, fc, :])
                    k.stt('dve', yc[:, :], pm_[:, :], -1.0 / 64, Yt[:, fc, :], ALU.mult, ALU.add)
                    k.act(sq[:, :], yc[:, :], AF.Square)
                    k.mm(pv_[:, :], BONER[:, :], sq[:, :])
                    k.act(rs[:, :], pv_[:, :], AF.Ln, bias=vcol('eps_gn', 0), scale=1.0 / 64)
                    k.act(rs[:, :], rs[:, :], AF.Exp, scale=-0.5)
                    k.tt('pool', yc[:, :], yc[:, :], rs[:, :], ALU.mult)
                    k.ts('pool', yc[:, :], yc[:, :], vcol('lnx_g', l, fc), vcol('lnx_b', l, fc), ALU.mult, ALU.add)
                    k.tt('dve', yc[:, :], yc[:, :], Bt[:, fc, :], ALU.add)
                    k.tt('dve', YG[:, fc, :], yc[:, :], Gt[:, fc, :], ALU.mult)
                k.ld(Yt[:, :, :], featv(HF, 0, 1024, t0, TT))
                k.ld(Bt[:, :, :], featv(HB, 0, 1024, t0, TT))
                k.ld(Gt[:, :, :], featv(Z, 5120, 1024, t0, TT))
                for br, wmat in ((0, w_out_a), (1, w_out_b)):
                    if br == 1:
                        k.act(Gt[:, :, :], Gt[:, :, :], AF.Silu)
                        k.tt('pool', Yt[:, :, :], Yt[:, :, :], Bt[:, :, :], ALU.add)
                        k.tt('dve', YG[:, :, :], Yt[:, :, :], Gt[:, :, :], ALU.mult)
                        if ti + 1 < len(tiles):
                            load_A(tiles[ti + 1][0])
                    for wq in range(4):
                        ws = WS[nw % 2]
                        nw += 1
                        wv = ws[:, :].re("p (k n) -> p k n", k=8)
                        k.ld(wv, wmat[l, :, wq * 512:(wq + 1) * 512].re("(k p) n -> p k n", p=128), eng='pool')
                        for j in range(4):
                            dc = wq * 4 + j
                            ps = PP[4 + dc % 2]
                            for fc in range(8):
                                k.mm(ps[:, :], wv[:, fc, j * 128:(j + 1) * 128], YG[:, fc, :], start=(fc == 0), stop=(fc == 7))
                            mch = MCH[nm_ % 2]
                            nm_ += 1
                            k.act(mch[:, :], MAH[dc // 8][:, dc % 8, :], AF.Sigmoid)
                            if br == 0 and dc % 8 == 7:
                                load_M(1, dc // 8, t0)
                            if br == 0:
                                k.tt('dve', (MGF[:, dc, :] if MGF is not None else MG[:, dc, :]), ps[:, :], mch[:, :], ALU.mult)
                            else:
                                k.tt('dve', mch[:, :], ps[:, :], mch[:, :], ALU.mult)
                                k.tt('pool', MG[:, dc, :], mgf(dc), mch[:, :], ALU.add)
                for oc_ in range(2):
                    k.ld(XCH[oc_][:, :], XIN[oc_ * 128:(oc_ + 1) * 128, t0:t0 + TT])
                for wq in range(8):
                    ws = WS[nw % 2]
                    nw += 1
                    wv = ws[:, :].re("p (k n) -> p k n", k=16)
                    k.ld(wv, w_out[l, :, wq * 256:(wq + 1) * 256].re("(k p) n -> p k n", p=128), eng='pool')
                    for j in range(2):
                        oc = wq * 2 + j
                        ps = PP[4 + oc % 2]
                        for dc in range(16):
                            k.mm(ps[:, :], wv[:, dc, j * 128:(j + 1) * 128], MG[:, dc, :], start=(dc == 0), stop=(dc == 15))
                        xch, xns = XCH[oc % 3], XNS[oc % 2]
                        if oc + 2 < 16:
                            k.ld(XCH[(oc + 2) % 3][:, :], XIN[(oc + 2) * 128:(oc + 3) * 128, t0:t0 + TT])
                        k.stt('dve', xns[:, :], ps[:, :], MOD[:, 32 + oc, ci:ci + 1], xch[:, :], ALU.mult, ALU.add)
                        k.ld(XOUT[oc * 128:(oc + 1) * 128, t0:t0 + TT], xns[:, :])
        S.barrier()

    with ExitStack() as st:
        XB = [sb(st, "XB5%d" % i, [128, 16, TT]) for i in range(2)]
        SQ = [sb(st, "SQ5%d" % i, [128, TT], F32R) for i in range(2)]
        RSTD = sb(st, "RSTD5", [128, TT])
        XNn = sb(st, "XNn", [128, 16, TT])
        YTK = [sb(st, "YTK%d" % i, [128, 4, D]) for i in range(2)]
        PSS = pt(st, "PSS5", 512)
        PT5 = [pt(st, "PT5%d" % i, 512) for i in range(4)]
        for ti, (t0, ci) in enumerate(tiles):
            xb = XB[ti % 2]
            yt = YTK[ti % 2]
            k.ld(xb[:, :, :], featv(XT[L], 0, D, t0, TT))
            rms_to(st, xb, PSS, RSTD, SQ)
            for dc in range(16):
                k.stt('dve' if dc % 2 else 'pool', XNn[:, dc, :], xb[:, dc, :], vcol('final_g', 0, dc), RSTD[:, :], ALU.mult, ALU.mult)
            n5 = 0
            for s_ in range(4):
                for dq in range(4):
                    ps = PT5[n5 % 4]
                    n5 += 1
                    for j in range(4):
                        dc = dq * 4 + j
                        k.tr(ps[:, j * 128:(j + 1) * 128], XNn[:, dc, s_ * 128:(s_ + 1) * 128], IDENT)
                    k.cp('act' if dq % 2 else 'dve', yt[:, s_, dq * 512:(dq + 1) * 512], ps[:, :])
            k.ld(y_all[t0:t0 + TT, :].re("(s p) d -> p s d", p=128), yt[:, :, :])
    S.barrier()
    S.emit(es)
    es.close()
    return nc


def make_consts():
    c = np.zeros((128, 8, 512), np.float32)
    p = np.arange(128)[:, None]
    f = np.arange(128)[None, :]
    ident = (p == f).astype(np.float32)
    bone = ((p // 64) == (f // 64)).astype(np.float32)
    gt = (f > p).astype(np.float32)
    ge = (f >= p).astype(np.float32)
    lt = (f < p).astype(np.float32)
    le = (f <= p).astype(np.float32)
    for i, m in enumerate((ident, bone, gt, ge, lt, le, -gt, -lt)):
        c[:, i, :] = np.tile(m, (1, 4))
    return c


def pack_vecs(inp):
    v = np.zeros((128, NVEC), np.float32)

    def put(name, l, arr):
        a = pm(arr)
        o = VOFF[(name, l)]
        v[:, o:o + a.shape[1]] = a
    for l in range(L):
        put('norm_g', l, inp['norm_g'][l])
        put('ada_b', l, inp['ada_b'][l])
        for j, n in enumerate(('mu_r', 'mu_k', 'mu_v')):
            put(n, l, inp['mu_rkv'][l, j])
        for d_ in range(2):
            put('dec_w0_%d' % d_, l, inp['dec_w0'][l, d_])
            put('iclr_w0_%d' % d_, l, inp['iclr_w0'][l, d_])
            put('gr_b_%d' % d_, l, inp['gr_b'][l, d_])
            put('gi_b_%d' % d_, l, inp['gi_b'][l, d_])
            put('lam_%d' % d_, l, inp['lru_lambda'][l, d_])
        if l > 0:
            put('vres_w0', l, inp['vres_w0'][l - 1])
        put('k_k', l, inp['k_k'][l])
        put('k_a', l, inp['k_a'][l])
        put('r_k', l, inp['r_k'][l].reshape(-1))
        put('lnx_g', l, inp['lnx_g'][l])
        put('lnx_b', l, inp['lnx_b'][l])
        for j in range(4):
            put('conv_w%d' % j, l, inp['conv_w'][l, j])
        put('conv_b', l, inp['conv_b'][l])
    put('final_g', 0, inp['final_g'])
    v[:, VOFF[('eps_rms', 0)]] = RMS_EPS
    v[:, VOFF[('eps_gn', 0)]] = GN_EPS
    return v


WNAMES = ['ada_w', 'w_in', 'w_out_a', 'w_out_b', 'w_out', 'dec_w1', 'dec_w2', 'iclr_w1', 'iclr_w2', 'vres_w1',
          'vres_w2', 'gr_w', 'gi_w']


def run(inp, n_cores, T_S, debug=False):
    inp = {k_: np.asarray(v_) for k_, v_ in inp.items()}
    nc = bass.Bass("TRN2", target_bir_lowering=False)
    build(nc, T_S, debug)
    consts = make_consts()
    vecs = pack_vecs(inp)
    shared = {n: np.ascontiguousarray(inp[n], dtype=np.float32) for n in WNAMES}
    in_maps = []
    for b in range(n_cores):
        xa = np.concatenate([inp['x_sample'][b], inp['x_prompt'][2 * b], inp['x_prompt'][2 * b + 1]], axis=0)
        cond = np.stack([inp['c'][b], inp['c_ctx']], axis=0)
        condT = np.ascontiguousarray(cond.reshape(2, 16, 128).transpose(2, 1, 0))
        st = inp['state_rwkv'][b]
        s0T = np.ascontiguousarray(st.transpose(0, 1, 2, 4, 3).reshape(L, 2, 8, 128, 64))
        h0 = np.ascontiguousarray(inp['state_lru'][b].reshape(L, 2, 8, 128).transpose(0, 1, 3, 2))
        m = dict(x_all=np.ascontiguousarray(xa, dtype=np.float32), condT=condT.astype(np.float32), s0T=s0T.astype(np.float32),
                 h0=h0.astype(np.float32), vecs=vecs, consts=consts)
        m.update(shared)
        in_maps.append(m)
    res = run_bass_kernel_spmd(nc, in_maps, core_ids=list(range(n_cores)))
    return res.results


def kernel(**inputs):
    T_S = 4096
    r = run(inputs, 8, T_S)
    y_prompt = np.zeros((16, 256, D), np.float32)
    y_sample = np.zeros((8, T_S, D), np.float32)
    nsr = np.zeros((16, L, 2, 16, 64, 64), np.float32)
    nsl = np.zeros((16, L, 2, 1024), np.float32)
    for b in range(8):
        ya = r[b]['y_all']
        y_sample[b] = ya[:T_S]
        y_prompt[2 * b] = ya[T_S:T_S + 256]
        y_prompt[2 * b + 1] = ya[T_S + 256:T_S + 512]
        for pi in range(2):
            s = r[b]['ns_rwkv'][pi]
            s = s.reshape(L, 2, 2, 64, 8, 64).transpose(0, 1, 4, 2, 5, 3)
            nsr[2 * b + pi] = s.reshape(L, 2, 16, 64, 64)
            hl = r[b]['ns_lru'][pi]
            nsl[2 * b + pi] = hl.transpose(0, 1, 3, 2).reshape(L, 2, 1024)
    return (y_prompt, y_sample, nsr, nsl)
```

```python
import numpy as np
from contextlib import ExitStack
import concourse.bass as bass
import concourse.mybir as mybir
from concourse.bass_utils import run_bass_kernel_spmd

F32 = mybir.dt.float32
F32R = mybir.dt.float32r
BF16 = mybir.dt.bfloat16
MMDT = BF16
AF = mybir.ActivationFunctionType
ALU = mybir.AluOpType

D = 2048
L = 2
NIN = 10240
RMS_EPS = 1e-6
GN_EPS = 64e-5
EPOCH = 30000
NSLOT = {'sp': 40, 'pool': 24}

VEC_SPEC = [('norm_g', 16), ('ada_b', 48), ('mu_r', 8), ('mu_k', 8), ('mu_v', 8), ('dec_w0_0', 8), ('dec_w0_1', 8),
            ('iclr_w0_0', 8), ('iclr_w0_1', 8), ('vres_w0', 8), ('k_k', 8), ('k_a', 8), ('r_k', 8), ('lnx_g', 8),
            ('lnx_b', 8), ('conv_w0', 8), ('conv_w1', 8), ('conv_w2', 8), ('conv_w3', 8), ('conv_b', 8),
            ('gr_b_0', 8), ('gr_b_1', 8), ('gi_b_0', 8), ('gi_b_1', 8), ('lam_0', 8), ('lam_1', 8)]
VOFF = {}
_o = 0
for _l in range(L):
    for _n, _c in VEC_SPEC:
        VOFF[(_n, _l)] = _o
        _o += _c
VOFF[('final_g', 0)] = _o
_o += 16
VOFF[('eps_rms', 0)] = _o
_o += 1
VOFF[('eps_gn', 0)] = _o
_o += 1
NVEC = _o


def pm(v):
    v = np.asarray(v, np.float32).reshape(-1, 128)
    return np.ascontiguousarray(v.T)


class Buf:
    def __init__(s, t, name, excl=False):
        s.t = t
        s.name = name
        s.w = None
        s.r = {}
        s.excl = excl

    def __getitem__(s, k):
        return V(s.t[k], [s])


class V:
    def __init__(s, ap, bufs):
        s.ap = ap
        s.bufs = bufs

    def bc(s, dt):
        return V(s.ap.bitcast(dt), s.bufs)

    def re(s, pat, **kw):
        return V(s.ap.rearrange(pat, **kw), s.bufs)

    def bcast(s, axis, shape):
        return V(s.ap.unsqueeze(axis).broadcast_to(list(shape)), s.bufs)

    def bto(s, shape):
        return V(s.ap.broadcast_to(list(shape)), s.bufs)

    def __getitem__(s, k):
        return V(s.ap[k], s.bufs)


class Sched:
    def __init__(s, nc):
        s.nc = nc
        s.ops = {e: [] for e in ('pe', 'act', 'dve', 'pool', 'sp')}
        s.cnt = {e: 0 for e in s.ops}
        s.waited = {e: {} for e in s.ops}
        s.slot = {'sp': 0, 'pool': 0}
        s.dcnt = {}
        s.keys = set()
        s.last = {}

    def _deps(s, e, r, w):
        toks = []
        for b in r:
            if b.w is not None:
                toks.append(b.w)
            if b.excl:
                toks.extend(t_ for t_ in b.r.values() if t_[2] != e)
        for b in w:
            if b.w is not None:
                toks.append(b.w)
            toks.extend(b.r.values())
        waits = []
        for (key, val, te, kind) in toks:
            if kind == 'c' and te == e and e == 'pe':
                continue
            if s.waited[e].get(key, 0) >= val:
                continue
            s.waited[e][key] = val
            waits.append((key, val))
        return waits

    def _upd(s, tok, r, w):
        for b in r:
            b.r[tok[0]] = tok
        for b in w:
            b.w = tok
            b.r = {}
        s.last[tok[0]] = tok

    def op(s, e, fn, r=(), w=()):
        waits = s._deps(e, r, w)
        ep, v = divmod(s.cnt[e], EPOCH)
        s.cnt[e] += 1
        key = ('c', e, ep)
        s.keys.add(key)
        tok = (key, v + 1, e, 'c')
        s.ops[e].append((waits, fn, key, 1))
        s._upd(tok, r, w)

    def dma(s, e, fn, r=(), w=()):
        waits = s._deps(e, r, w)
        sl = s.slot[e]
        s.slot[e] = (sl + 1) % NSLOT[e]
        key = ('d', e, sl)
        s.keys.add(key)
        prev = s.dcnt.get(key, 0)
        if prev > 0 and s.waited[e].get(key, 0) < prev:
            s.waited[e][key] = prev
            waits.append((key, prev))
        s.dcnt[key] = prev + 16
        tok = (key, s.dcnt[key], e, 'd')
        s.ops[e].append((waits, fn, key, 16))
        s._upd(tok, r, w)

    def barrier(s):
        toks = list(s.last.values())
        for e in s.ops:
            waits = []
            for (key, val, te, kind) in toks:
                if kind == 'c' and te == e:
                    continue
                if s.waited[e].get(key, 0) >= val:
                    continue
                s.waited[e][key] = val
                waits.append((key, val))
            if waits:
                s.ops[e].append((waits, None, None, 0))

    def emit(s, es):
        nc = s.nc
        sems = {}
        for i, key in enumerate(sorted(s.keys)):
            sems[key] = es.enter_context(nc.semaphore("s%d" % i))
        with nc.Block() as block:
            def run(e):
                def f(eng):
                    for waits, fn, key, inc in s.ops[e]:
                        for (k, v) in waits:
                            eng.wait_ge(sems[k], v)
                        if fn is not None:
                            fn(eng).then_inc(sems[key], inc)
                return f
            block.sync(run('sp'))
            block.tensor(run('pe'))
            block.scalar(run('act'))
            block.vector(run('dve'))
            block.gpsimd(run('pool'))


def _bufs(*vs):
    out = []
    for v in vs:
        if isinstance(v, V):
            for b in v.bufs:
                if b not in out:
                    out.append(b)
    return out


def _ap(x):
    return x.ap if isinstance(x, V) else x


class K:
    def __init__(s, S):
        s.S = S

    def mm(s, out, lhsT, rhs, start=True, stop=True):
        s.S.op('pe', lambda e: e.matmul(out.ap, lhsT.ap, rhs.ap, start=start, stop=stop),
               _bufs(lhsT, rhs), _bufs(out))

    def tr(s, out, in_, ident):
        s.S.op('pe', lambda e: e.transpose(out.ap, in_.ap, ident.ap), _bufs(in_, ident), _bufs(out))

    def act(s, out, in_, func, bias=0.0, scale=1.0):
        s.S.op('act', lambda e: e.activation(out=out.ap, in_=in_.ap, func=func, bias=_ap(bias), scale=_ap(scale)),
               _bufs(in_, bias, scale), _bufs(out))

    def tt(s, eng, out, a, b, op):
        s.S.op(eng, lambda e: e.tensor_tensor(out.ap, a.ap, b.ap, op), _bufs(a, b), _bufs(out))

    def ts(s, eng, out, a, s1, s2, op0, op1=None):
        if op1 is None:
            s.S.op(eng, lambda e: e.tensor_scalar(out.ap, a.ap, _ap(s1), None, op0), _bufs(a, s1), _bufs(out))
        else:
            s.S.op(eng, lambda e: e.tensor_scalar(out.ap, a.ap, _ap(s1), _ap(s2), op0, op1),
                   _bufs(a, s1, s2), _bufs(out))

    def stt(s, eng, out, a, sc, b, op0, op1):
        eng = 'dve'
        s.S.op(eng, lambda e: e.scalar_tensor_tensor(out.ap, a.ap, _ap(sc), b.ap, op0, op1),
               _bufs(a, sc, b), _bufs(out))

    def cp(s, eng, out, in_):
        if eng == 'act':
            s.S.op('act', lambda e: e.copy(out.ap, in_.ap), _bufs(in_), _bufs(out))
        else:
            s.S.op(eng, lambda e: e.tensor_copy(out.ap, in_.ap), _bufs(in_), _bufs(out))

    def rcp(s, out, in_):
        s.S.op('dve', lambda e: e.reciprocal(out.ap, in_.ap), _bufs(in_), _bufs(out))

    def ms(s, eng, out, val):
        s.S.op(eng, lambda e: e.memset(out.ap, val), (), _bufs(out))

    def scan(s, out, d0, d1, init):
        s.S.op('dve', lambda e: e.tensor_tensor_scan(out.ap, d0.ap, d1.ap, _ap(init), ALU.mult, ALU.add),
               _bufs(d0, d1, init), _bufs(out))

    def ld(s, out, in_, eng='sp'):
        s.S.dma(eng, lambda e: e.dma_start(out=out.ap, in_=in_.ap), _bufs(in_), _bufs(out))


def build(nc, T_S, debug=False):
    TT = 512
    Ttot = T_S + 512
    NT = Ttot // TT
    NTS = T_S // TT
    NCS = T_S // 128
    es = ExitStack()
    S = Sched(nc)
    k = K(S)

    def din(name, shape):
        return Buf(nc.dram_tensor(name, list(shape), F32, kind="ExternalInput").ap(), name)

    def dout(name, shape):
        return Buf(nc.dram_tensor(name, list(shape), F32, kind="ExternalOutput").ap(), name)

    def dscr(name, shape):
        return Buf(nc.dram_tensor(name, list(shape), F32, kind="ExternalOutput" if debug else "Internal").ap(), name)

    x_all = din("x_all", [Ttot, D])
    condT = din("condT", [128, 16, 2])
    s0T = din("s0T", [L, 2, 8, 128, 64])
    h0 = din("h0", [L, 2, 128, 8])
    vecs = din("vecs", [128, NVEC])
    consts = din("consts", [128, 8, 512])
    ada_w = din("ada_w", [L, D, 3 * D])
    w_in = din("w_in", [L, D, NIN])
    w_out_a = din("w_out_a", [L, 1024, D])
    w_out_b = din("w_out_b", [L, 1024, D])
    w_out = din("w_out", [L, D, D])
    dec_w1 = din("dec_w1", [L, 2, D, 96])
    dec_w2 = din("dec_w2", [L, 2, 96, 1024])
    iclr_w1 = din("iclr_w1", [L, 2, D, 96])
    iclr_w2 = din("iclr_w2", [L, 2, 96, 1024])
    vres_w1 = din("vres_w1", [1, D, 64])
    vres_w2 = din("vres_w2", [1, 64, 1024])
    gr_w = din("gr_w", [L, 2, 16, 64, 64])
    gi_w = din("gi_w", [L, 2, 16, 64, 64])

    y_all = dout("y_all", [Ttot, D])
    ns_rwkv = dout("ns_rwkv", [2, L, 2, 128, 8, 64])
    ns_lru = dout("ns_lru", [2, L, 2, 128, 8])

    XT = [dscr("XT%d" % i, [D, Ttot]) for i in range(3)]
    Z = dscr("Z", [NIN, Ttot])
    LOR = dscr("LOR", [5 * 1024, Ttot])
    Z2 = dscr("Z2", [4 * 1024, Ttot])
    VF = dscr("VF", [1024, Ttot])
    YY = dscr("YY", [1024, Ttot])
    BO = dscr("BO", [1024, Ttot])
    HF = dscr("HF", [1024, Ttot])
    HB = dscr("HB", [1024, Ttot])

    uid = [0]

    def sb(st, name, shape, dt=F32):
        uid[0] += 1
        name = "%s_%d" % (name, uid[0])
        return Buf(st.enter_context(nc.sbuf_tensor(name, list(shape), dt)), name)

    def pt(st, name, cols):
        uid[0] += 1
        name = "%s_%d" % (name, uid[0])
        return Buf(st.enter_context(nc.psum_tensor(name, [128, cols], F32)), name, excl=True)

    VEC = sb(es, "VEC", [128, NVEC])
    CST = sb(es, "CST", [128, 8, 512])
    ONESR = sb(es, "ONESR", [128, 128], F32R)
    BONE = sb(es, "BONE", [128, 128])
    BONER = sb(es, "BONER", [128, 128], F32R)
    MOD = sb(es, "MOD", [128, 48, 2])
    GS = sb(es, "GS", [128, 16, 2])
    ONE1 = sb(es, "ONE1", [128, 128])
    k.ld(VEC[:, :], vecs[:, :])
    k.ld(CST[:, :, :], consts[:, :, :])
    ZER = sb(es, "ZER", [128, 128])
    k.ms('pool', ZER[:, :], 0.0)
    k.ms('pool', ONE1[:, 0:128], 1.0 / D)
    k.cp('dve', ONESR[:, :], ONE1[:, 0:128])
    k.ms('pool', ONE1[:, :], 1.0)
    k.cp('dve', BONE[:, :], CST[:, 1, 0:128])
    k.cp('dve', BONER[:, :], CST[:, 1, 0:128])
    IDENT = CST[:, 0, 0:128]

    def vcol(name, l, j=0, n=1):
        o = VOFF[(name, l)] + j
        return VEC[:, o:o + n]

    tiles = []
    for i in range(NTS):
        tiles.append((i * TT, 0))
    tiles.append((T_S, 1))
    seqs = [(0, T_S, 'grid'), (T_S, 256, 'p0'), (T_S + 256, 256, 'p1')]

    def featv(dbuf, row0, nrows, t0, n):
        return dbuf[row0:row0 + nrows, t0:t0 + n].re("(c p) t -> p c t", p=128)

    with ExitStack() as st:
        XTOK = [sb(st, "XTOK%d" % i, [128, 4, D]) for i in range(2)]
        XTB = [sb(st, "XTB%d" % i, [128, 16, TT]) for i in range(2)]
        PSs = [pt(st, "P0ps%d" % i, 512) for i in range(4)]
        for ti, (t0, ci) in enumerate(tiles):
            xt = XTOK[ti % 2]
            xb = XTB[ti % 2]
            k.ld(xt[:, :, :], x_all[t0:t0 + TT, :].re("(s p) d -> p s d", p=128))
            for dc in range(16):
                ps = PSs[dc % 4]
                for s_ in range(4):
                    k.tr(ps[:, s_ * 128:(s_ + 1) * 128], xt[:, s_, dc * 128:(dc + 1) * 128], IDENT)
                k.cp('act' if dc % 2 else 'dve', xb[:, dc, :], ps[:, :])
            k.ld(featv(XT[0], 0, D, t0, TT), xb[:, :, :])
    S.barrier()

    def rms_to(st_, xb, PSS, RSTD, SQ):
        for dc in range(16):
            sq = SQ[dc % 2]
            k.act(sq[:, :], xb[:, dc, :], AF.Square)
            k.mm(PSS[:, :], ONESR[:, :], sq[:, :], start=(dc == 0), stop=(dc == 15))
        k.act(RSTD[:, :], PSS[:, :], AF.Ln, bias=vcol('eps_rms', 0), scale=1.0)
        k.act(RSTD[:, :], RSTD[:, :], AF.Exp, scale=-0.5)

    for l in range(L):
        XIN = XT[l]
        XOUT = XT[l + 1]
        with ExitStack() as st:
            SC = sb(st, "SC", [128, 16, 2])
            AW = [sb(st, "AW%d" % i, [128, 16, 512]) for i in range(2)]
            PA = pt(st, "PA", 512)
            k.ld(SC[:, :, :], condT[:, :, :])
            k.act(SC[:, :, :], SC[:, :, :], AF.Silu)
            for wt in range(12):
                aw = AW[wt % 2]
                k.ld(aw[:, :, :], ada_w[l, :, wt * 512:(wt + 1) * 512].re("(k p) n -> p k n", p=128))
                for j in range(4):
                    oc = wt * 4 + j
                    for kc in range(16):
                        k.mm(PA[:, oc * 2:oc * 2 + 2], aw[:, kc, j * 128:(j + 1) * 128], SC[:, kc, :],
                             start=(kc == 0), stop=(kc == 15))
            for c in range(2):
                k.tt('dve', MOD[:, :, c], PA[:, 0:96].re("p (o c) -> p o c", c=2)[:, :, c], vcol('ada_b', l, 0, 48), ALU.add)
            for c in range(2):
                k.stt('dve', GS[:, :, c], MOD[:, 16:32, c], 1.0, vcol('norm_g', l, 0, 16), ALU.add, ALU.mult)
        S.barrier()

        with ExitStack() as st:
            GW = 2 * TT
            XB = [sb(st, "XB%d" % i, [128, 16, TT]) for i in range(2)]
            XM = sb(st, "XM", [128, 16, GW], MMDT)
            WB = [sb(st, "WB%d" % i, [128, 16, 256], MMDT) for i in range(2)]
            ZST = [sb(st, "ZST%d" % i, [128, 2, GW]) for i in range(2)]
            LST = sb(st, "LST", [128, 8, TT])
            LW1 = [sb(st, "LW1", [128, 16, 96], MMDT)] * 2
            LW2 = [sb(st, "LW2", [96, 1024], F32R)] * 2
            T1 = sb(st, "T1", [96, TT], F32R)
            SQ = [sb(st, "SQ%d" % i, [128, TT], F32R) for i in range(2)]
            RSTD = sb(st, "RSTD", [128, TT])
            TMP = [sb(st, "TMP%d" % i, [128, TT]) for i in range(2)]
            PSS = pt(st, "PSS", 512)
            PZ = [pt(st, "PZ%d" % i, 512) for i in range(4)]
            PL = [pt(st, "PL%d" % i, 512) for i in range(3)]
            loras = [('dec', 0, dec_w1, dec_w2, 96, 'dec_w0_0'), ('dec', 1, dec_w1, dec_w2, 96, 'dec_w0_1'),
                     ('iclr', 0, iclr_w1, iclr_w2, 96, 'iclr_w0_0'), ('iclr', 1, iclr_w1, iclr_w2, 96, 'iclr_w0_1')]
            if l > 0:
                loras.append(('vres', 0, vres_w1, vres_w2, 64, 'vres_w0'))
            groups = []
            ti_ = 0
            while ti_ < len(tiles):
                t0_, ci_ = tiles[ti_]
                if ci_ == 0 and ti_ + 1 < len(tiles) and tiles[ti_ + 1][1] == 0:
                    groups.append((t0_, 2, ci_))
                    ti_ += 2
                else:
                    groups.append((t0_, 1, ci_))
                    ti_ += 1
            nl = 0
            nxb = 0
            npz = 0
            for (g0, nh, ci) in groups:
                for hf in range(nh):
                    xb = XB[nxb % 2]
                    nxb += 1
                    k.ld(xb[:, :, :], featv(XIN, 0, D, g0 + hf * TT, TT))
                    rms_to(st, xb, PSS, RSTD, SQ)
                    for dc in range(16):
                        tmp = TMP[dc % 2]
                        k.stt('dve', tmp[:, :], xb[:, dc, :], GS[:, dc, ci:ci + 1], RSTD[:, :], ALU.mult, ALU.mult)
                        k.act(XM[:, dc, hf * TT:(hf + 1) * TT], tmp[:, :], AF.Identity, bias=MOD[:, dc, ci:ci + 1], scale=1.0)
                for wt in range(NIN // 256):
                    wb = WB[wt % 2]
                    k.ld(wb[:, :, :], w_in[l, :, wt * 256:(wt + 1) * 256].re("(k p) n -> p k n", p=128), eng='pool')
                    zs = ZST[wt % 2]
                    for j in range(2):
                        for hf in range(nh):
                            ps = PZ[npz % 4]
                            npz += 1
                            for kc in range(16):
                                k.mm(ps[:, :], wb[:, kc, j * 128:(j + 1) * 128], XM[:, kc, hf * TT:(hf + 1) * TT],
                                     start=(kc == 0), stop=(kc == 15))
                            k.cp('act' if (j + hf) % 2 else 'dve', zs[:, j, hf * TT:(hf + 1) * TT], ps[:, :])
                    k.ld(featv(Z, wt * 256, 256, g0, nh * TT), zs[:, :, 0:nh * TT])
                for li, (nm, d_, w1, w2, R, w0n) in enumerate(loras):
                    lw1 = LW1[nl % 2]
                    lw2 = LW2[nl % 2]
                    nl += 1
                    if nm == 'vres':
                        k.ld(lw1[:, :, 0:R], w1[0, :, :].re("(k p) r -> p k r", p=128), eng='pool')
                        k.ld(lw2[0:R, :], w2[0, :, :], eng='pool')
                    else:
                        k.ld(lw1[:, :, 0:R], w1[l, d_, :, :].re("(k p) r -> p k r", p=128), eng='pool')
                        k.ld(lw2[0:R, :], w2[l, d_, :, :], eng='pool')
                    for hf in range(nh):
                        p1 = PL[2]
                        for kc in range(16):
                            k.mm(p1[0:R, :], lw1[:, kc, 0:R], XM[:, kc, hf * TT:(hf + 1) * TT], start=(kc == 0), stop=(kc == 15))
                        if nm == 'dec':
                            k.act(T1[0:R, :], p1[0:R, :], AF.Tanh)
                        else:
                            k.cp('dve', T1[0:R, :], p1[0:R, :])
                        for oc in range(8):
                            p2 = PL[oc % 2]
                            k.mm(p2[:, :], lw2[0:R, oc * 128:(oc + 1) * 128], T1[0:R, :])
                            k.act(LST[:, oc, :], p2[:, :], AF.Sigmoid, bias=vcol(w0n, l if nm != 'vres' else 1, oc), scale=1.0)
                        k.ld(featv(LOR, li * 1024, 1024, g0 + hf * TT, TT), LST[:, :, :])
        S.barrier()

        with ExitStack() as st:
            BIN = [sb(st, "BIN%d" % i, [128, 8, 640]) for i in range(2)]
            ACCs = [sb(st, "ACC%d" % i, [128, 8, TT]) for i in range(2)]
            MIX = [sb(st, "MIX%d" % i, [128, 8, TT]) for i in range(3)]
            VFt = sb(st, "VFt", [128, 8, TT])
            VGt = sb(st, "VGt", [128, 8, TT])
            KK = sb(st, "KK", [128, 8, TT])
            SQK = [sb(st, "SQK%d" % i, [128, TT], F32R) for i in range(2)]
            RN = [sb(st, "RN%d" % i, [128, TT]) for i in range(2)]
            KAP = KK
            PK = [pt(st, "PK%d" % i, 512) for i in range(2)]
            nb = 0
            for ti, (t0, ci) in enumerate(tiles):
                for q, mun in enumerate(('mu_r', 'mu_k', 'mu_v')):
                    B = BIN[nb % 2]
                    ACC = ACCs[nb % 2]
                    nb += 1
                    mix = MIX[q]
                    if ci == 0:
                        lo = t0 - 64 if t0 > 0 else t0
                        hi = t0 + TT + 64 if t0 + TT < T_S else t0 + TT
                        if t0 == 0:
                            k.ms('pool', B[:, :, 0:64], 0.0)
                        if t0 + TT >= T_S:
                            k.ms('pool', B[:, :, 576:640], 0.0)
                        k.ld(B[:, :, 64 - (t0 - lo):576 + (hi - t0 - TT)], featv(Z, q * 1024, 1024, lo, hi - lo))
                        C = B[:, :, 64:576]
                        k.tt('pool', ACC[:, 0:3, :], B[:, 0:3, 0:512], B[:, 0:3, 128:640], ALU.add)
                        k.tt('dve', ACC[:, 3:8, :], B[:, 3:8, 0:512], B[:, 3:8, 128:640], ALU.add)
                        C4 = C.re("p c (r w) -> p c r w", w=64)
                        A4 = ACC[:, :, :].re("p c (r w) -> p c r w", w=64)
                        k.tt('dve', A4[:, :, :, 1:64], A4[:, :, :, 1:64], C4[:, :, :, 0:63], ALU.add)
                        k.tt('dve', A4[:, :, :, 0:63], A4[:, :, :, 0:63], C4[:, :, :, 1:64], ALU.add)
                        sc = 0.25
                    else:
                        k.ld(B[:, :, 64:576], featv(Z, q * 1024, 1024, t0, TT))
                        C = B[:, :, 64:576]
                        C4 = C.re("p c (s w) -> p c s w", w=256)
                        A4 = ACC[:, :, :].re("p c (s w) -> p c s w", w=256)
                        k.ms('pool', A4[:, :, :, 0:1], 0.0)
                        k.cp('pool', A4[:, :, :, 1:256], C4[:, :, :, 0:255])
                        k.tt('dve', A4[:, :, :, 0:255], A4[:, :, :, 0:255], C4[:, :, :, 1:256], ALU.add)
                        sc = 0.5
                    k.stt('dve', ACC[:, :, :], ACC[:, :, :], sc, C, ALU.mult, ALU.subtract)
                    for fc in range(8):
                        k.stt('pool' if fc % 2 else 'dve', mix[:, fc, :], ACC[:, fc, :], vcol(mun, l, fc), C[:, fc, :], ALU.mult, ALU.add)
                    if q == 0:
                        k.ld(featv(Z2, 0, 1024, t0, TT), mix[:, :, :])
                    elif q == 1:
                        k.ld(featv(Z2, 1024, 1024, t0, TT), mix[:, :, :])
                        for fc in range(8):
                            k.act(KK[:, fc, :], mix[:, fc, :], AF.Identity, bias=0.0, scale=vcol('k_k', l, fc))
                        for fc in range(8):
                            sq = SQK[fc % 2]
                            rn = RN[fc % 2]
                            k.act(sq[:, :], KK[:, fc, :], AF.Square)
                            k.mm(PK[fc % 2][:, :], BONER[:, :], sq[:, :])
                            k.ts('dve', rn[:, :], PK[fc % 2][:, :], 1e-24, None, ALU.max)
                            k.act(rn[:, :], rn[:, :], AF.Ln)
                            k.act(rn[:, :], rn[:, :], AF.Exp, scale=-0.5)
                            k.tt('dve', KAP[:, fc, :], KK[:, fc, :], rn[:, :], ALU.mult)
                        k.ld(featv(Z2, 3072, 1024, t0, TT), KAP[:, :, :])
                    else:
                        if l > 0:
                            k.ld(VFt[:, :, :], featv(VF, 0, 1024, t0, TT))
                            k.ld(VGt[:, :, :], featv(LOR, 4096, 1024, t0, TT))
                            k.tt('pool', VFt[:, :, :], VFt[:, :, :], mix[:, :, :], ALU.subtract)
                            k.tt('dve', VFt[:, :, :], VFt[:, :, :], VGt[:, :, :], ALU.mult)
                            k.tt('dve', mix[:, :, :], mix[:, :, :], VFt[:, :, :], ALU.add)
                        else:
                            k.ld(featv(VF, 0, 1024, t0, TT), mix[:, :, :])
                        k.ld(featv(Z2, 2048, 1024, t0, TT), mix[:, :, :])
        S.barrier()

        with ExitStack() as st:
            def t4(name, dt=F32):
                return sb(st, name, [128, 8, 128], dt)
            LD0 = [t4("LD0_%d" % j) for j in range(6)]
            LD = [LD0, [LD0[j] if j != 2 else t4("LD1_2") for j in range(6)]]
            CL, CLN, EXPB, TMP2, BON = t4("CL"), t4("CLN"), t4("EXPB"), t4("TMP2"), t4("BON")
            QR = sb(st, "QR", [128, 8, 2, 128], F32R)
            KH, BH = t4("KH", F32R), t4("BH", F32R)
            KAb = vcol('k_a', l, 0, 8).bcast(2, [128, 8, 128])
            RKb = vcol('r_k', l, 0, 8).bcast(2, [128, 8, 128])
            VT_, KT_, BT_ = (sb(st, n, [128, 1024]) for n in ("VT_", "KT_", "BT_"))
            AKK, ARK, ARB = (sb(st, n, [128, 16, 128]) for n in ("AKK", "ARK", "ARB"))
            XN = sb(st, "XN", [128, 16, 128], F32R)
            XTPT = sb(st, "XTPT", [128, 16, 2, 128], F32R)
            XNq = [Buf(XN.t, "XNq%d" % q) for q in range(4)]
            XTPTq = [Buf(XTPT.t, "XTPTq%d" % q) for q in range(4)]
            RH, NU = sb(st, "RH", [128, 1024]), sb(st, "NU", [128, 1024])
            ST_, S0P = sb(st, "ST_", [128, 8, 64]), sb(st, "S0P", [128, 8, 64])
            YT = [t4("YT0")] * 2
            YP = [t4("YP0")] * 2
            BP = [t4("BP0")] * 2
            EI, EO = sb(st, "EI", [128, 8]), sb(st, "EO", [128, 8])
            PQa = [pt(st, "PQa%d" % i, 1024) for i in range(2)]
            PQb = [pt(st, "PQb%d" % i, 512) for i in range(2)]
            PQa3 = PQa + [pt(st, "PQa2", 1024)]
            ninv = [0]
            RSTM = [sb(st, "RSTM%d" % i, [128, 1024]) for i in range(2)]
            for i_ in range(2):
                k.ms('pool', RSTM[i_][:, :], 1.0)
                z_ = 0 if i_ == 0 else 127
                k.ms('pool', RSTM[i_][:, :].re("p (c t) -> p c t", t=128)[:, :, z_:z_ + 1], 0.0)

            def m4(typ):
                return CST[:, typ, :].re("p (h t) -> p h t", h=4)
            M_GT, M_GE, M_LT, M_LE, M_NGT, M_NLT = 2, 3, 4, 5, 6, 7
            nld = 0
            flat = []
            for d__ in range(2):
                for (sq0_, sqn_, kind_) in seqs:
                    n_ = sqn_ // 128
                    for ci_ in (range(n_) if d__ == 0 else range(n_ - 1, -1, -1)):
                        flat.append((d__, sq0_ + ci_ * 128))

            def issue_loads(i):
                dd, cc = flat[i]
                tl_ = LD[i % 2]
                for j in range(4):
                    k.ld(tl_[j][:, :, :], featv(Z2, j * 1024, 1024, cc, 128))
                k.ld(tl_[4][:, :, :], featv(LOR, (2 + dd) * 1024, 1024, cc, 128))
                k.ld(tl_[5][:, :, :], featv(LOR, dd * 1024, 1024, cc, 128))

            EI2 = [sb(st, "EI2%d" % i, [128, 8]) for i in range(2)]

            def prepA_pool1(i):
                r_, k_, v_, kp_, a_, sg_ = LD[i % 2]
                k.ts('pool', TMP2[:, :, :], a_[:, :, :], -1.0, None, ALU.add)
                k.tt('pool', TMP2[:, :, :], TMP2[:, :, :], KAb, ALU.mult)

            def prepA_dve(i):
                dd, cc = flat[i]
                r_, k_, v_, kp_, a_, sg_ = LD[i % 2]
                md = 63 if dd == 0 else 64
                k.stt('dve', k_[:, :, :], TMP2[:, :, :], 1.0, k_[:, :, :], ALU.add, ALU.mult)
                k.ts('dve', sg_[:, :, :], sg_[:, :, :], -float(np.exp(-0.5)), None, ALU.mult)
                clf = CL[:, :, :].re("p c t -> p (c t)")
                sgf = sg_[:, :, :].re("p c t -> p (c t)")
                if dd == 0:
                    k.scan(clf, RSTM[0][:, :], sgf, 0.0)
                else:
                    k.scan(clf[:, ::-1], RSTM[1][:, ::-1], sgf[:, ::-1], 0.0)
                k.tt('dve', CLN[:, :, :], CL[:, :, :], CL[:, :, md:md + 1].bto([128, 8, 128]), ALU.subtract)
                k.act(EI2[i % 2][:, :], CL[:, :, md], AF.Exp)

            QRs = [QR, sb(st, "QRb", [128, 8, 2, 128], F32R)]
            KHs = [KH, t4("KHb", F32R)]
            BHs = [BH, t4("BHb", F32R)]
            EOs = [EO, sb(st, "EOb", [128, 8])]
            TMP3 = t4("TMP3")

            def prepB_act(i):
                dd, cc = flat[i]
                r_, k_, v_, kp_, a_, sg_ = LD[i % 2]
                lst = 127 if dd == 0 else 0
                k.tt('pool', TMP3[:, :, :], CLN[:, :, :], sg_[:, :, :], ALU.subtract)
                k.act(EXPB[:, :, :], CLN[:, :, :], AF.Exp)
                k.cp('act', EOs[i % 2][:, :], EXPB[:, :, lst])
                k.act(CL[:, :, :], TMP3[:, :, :], AF.Exp)
                k.act(TMP3[:, :, :], CLN[:, :, :], AF.Exp, scale=-1.0)

            def prepB_mul(i):
                r_, k_, v_, kp_, a_, sg_ = LD[i % 2]
                k.tt('dve', QRs[i % 2][:, :, 1, :], r_[:, :, :], EXPB[:, :, :], ALU.mult)
                k.tt('dve', QRs[i % 2][:, :, 0, :], kp_[:, :, :], CL[:, :, :], ALU.mult)
                k.tt('pool', KHs[i % 2][:, :, :], k_[:, :, :], TMP3[:, :, :], ALU.mult)
                k.tt('pool', BHs[i % 2][:, :, :], a_[:, :, :], TMP3[:, :, :], ALU.mult)

            def prepA_pool2(i):
                r_, k_, v_, kp_, a_, sg_ = LD[i % 2]
                k.tt('pool', a_[:, :, :], kp_[:, :, :], a_[:, :, :], ALU.mult)
                k.tt('pool', TMP2[:, :, :], r_[:, :, :], k_[:, :, :], ALU.mult)
                k.tt('pool', TMP2[:, :, :], TMP2[:, :, :], RKb, ALU.mult)
            for d_ in range(2):
                if d_ == 0:
                    m_strT, m_incT, m_nstrT, m_nstrN = M_GT, M_GE, M_NGT, M_NLT
                    mid, last = 63, 127
                else:
                    m_strT, m_incT, m_nstrT, m_nstrN = M_LT, M_LE, M_NLT, M_NGT
                    mid, last = 64, 0
                for si, (sq0, sqn, kind) in enumerate(seqs):
                    nchs = sqn // 128
                    if kind == 'grid':
                        k.ld(ST_[:, :, :], s0T[l, d_, :, :, :].re("c p v -> p c v"))
                    else:
                        k.ms('pool', ST_[:, :, :], 0.0)
                    order = range(nchs) if d_ == 0 else range(nchs - 1, -1, -1)
                    for cidx in order:
                        c0 = sq0 + cidx * 128
                        R_, K_, V_, KP_, A_, SG_ = LD[nld % 2]
                        YTt, YPt, BPt = YT[nld % 2], YP[nld % 2], BP[nld % 2]
                        nld += 1
                        if nld == 1:
                            issue_loads(0)
                        if d_ == 1:
                            k.ld(YPt[:, :, :], featv(YY, 0, 1024, c0, 128))
                            k.ld(BPt[:, :, :], featv(BO, 0, 1024, c0, 128))
                        if nld == 1:
                            prepA_pool1(0)
                            prepA_dve(0)
                            prepA_pool2(0)
                        EI = EI2[(nld - 1) % 2]
                        pb = PQa[0]
                        for fc in range(8):
                            k.mm(pb[:, fc * 128:(fc + 1) * 128], BONE[:, :], TMP2[:, fc, :])
                        if d_ == 0:
                            k.tt('dve', BON[:, :, :], pb[:, :].re("p (c t) -> p c t", t=128), V_[:, :, :], ALU.mult)
                        else:
                            k.tt('dve', BON[:, :, :], pb[:, :].re("p (c t) -> p c t", t=128), V_[:, :, :], ALU.mult)
                            k.tt('pool', BON[:, :, :], BON[:, :, :], BPt[:, :, :], ALU.add)
                        k.ld(featv(BO, 0, 1024, c0, 128), BON[:, :, :])
                        if nld == 1:
                            prepB_act(0)
                            prepB_mul(0)
                        QR, KH, BH, EO = QRs[(nld - 1) % 2], KHs[(nld - 1) % 2], BHs[(nld - 1) % 2], EOs[(nld - 1) % 2]
                        if nld < len(flat):
                            issue_loads(nld)
                        for (src, dst, pq) in ((V_, VT_, PQa[1]), (KH, KT_, PQa[0]), (BH, BT_, PQa[1])):
                            for fc in range(8):
                                sv = src[:, fc, :]
                                if src is not V_:
                                    sv = sv.bc(F32)
                                k.tr(pq[:, fc * 128:(fc + 1) * 128], sv, IDENT)
                            k.cp('act' if dst is KT_ else 'dve', dst[:, :], pq[:, :])
                        for q in range(4):
                            for hq in range(4):
                                h = 4 * q + hq
                                fc, p0 = h // 2, (h % 2) * 64
                                qr = QR[p0:p0 + 64, fc, :, :].re("p a t -> p (a t)")
                                k.mm(PQa[0][:, hq * 256:(hq + 1) * 256], BH[p0:p0 + 64, fc, :], qr)
                                k.mm(PQa[1][:, hq * 256:(hq + 1) * 256], KH[p0:p0 + 64, fc, :], qr)
                                k.mm(PQb[0][:, hq * 128:(hq + 1) * 128], QR[p0:p0 + 64, fc, 0, :], BH[p0:p0 + 64, fc, :])
                            hs = slice(4 * q, 4 * q + 4)
                            a1 = PQa[0][:, :].re("p (h a t) -> p h a t", h=4, a=2)
                            a2 = PQa[1][:, :].re("p (h a t) -> p h a t", h=4, a=2)
                            a3 = PQb[0][:, :].re("p (h t) -> p h t", h=4)
                            k.tt('dve', XTPTq[q][:, hs, 0, :], a1[:, :, 0, :], m4(m_nstrT), ALU.mult)
                            k.tt('dve', ARB[:, hs, :], a1[:, :, 1, :], m4(m_incT), ALU.mult)
                            k.tt('dve', AKK[:, hs, :], a2[:, :, 0, :], m4(m_strT), ALU.mult)
                            k.tt('dve', ARK[:, hs, :], a2[:, :, 1, :], m4(m_incT), ALU.mult)
                            k.tt('dve', XNq[q][:, hs, :], a3[:, :, :], m4(m_nstrN), ALU.mult)
                            k.cp('pool', XTPTq[q][:, hs, 1, :], CST[:, 0, :].re("p (h t) -> p h t", h=4))
                        for step in range(7):
                            lastst = (step == 6)
                            for q in range(4):
                                pa, pbk = PQa3[ninv[0] % 3], PQb[ninv[0] % 2]
                                ninv[0] += 1
                                hs = slice(4 * q, 4 * q + 4)
                                for hq in range(4):
                                    h = 4 * q + hq
                                    if not lastst:
                                        k.mm(pa[:, hq * 256:(hq + 1) * 256], XNq[q][:, h, :],
                                             XTPTq[q][:, h, :, :].re("p a t -> p (a t)"))
                                        k.mm(pbk[:, hq * 128:(hq + 1) * 128], XTPTq[q][:, h, 0, :], XNq[q][:, h, :])
                                    else:
                                        k.mm(pa[:, hq * 256 + 128:(hq + 1) * 256], XNq[q][:, h, :], XTPTq[q][:, h, 1, :])
                                a1 = pa[:, :].re("p (h a t) -> p h a t", h=4, a=2)
                                k.tt('dve', XTPTq[q][:, hs, 1, :], a1[:, :, 1, :], XTPTq[q][:, hs, 1, :], ALU.add)
                                if not lastst:
                                    k.cp('dve', XTPTq[q][:, hs, 0, :], a1[:, :, 0, :])
                                    k.cp('act', XNq[q][:, hs, :], pbk[:, :].re("p (h t) -> p h t", h=4))
                            if nld < len(flat):
                                if step == 1:
                                    prepA_pool1(nld)
                                elif step == 3:
                                    prepA_dve(nld)
                                elif step == 4:
                                    prepB_act(nld)
                                elif step == 5:
                                    prepA_pool2(nld)
                                    prepB_mul(nld)
                        k.tt('dve', S0P[:, :, :], ST_[:, :, :], EI[:, :].bcast(2, [128, 8, 64]), ALU.mult)
                        pr = PQa[0]
                        for h in range(16):
                            fc, p0 = h // 2, (h % 2) * 64
                            k.mm(pr[:, h * 64:(h + 1) * 64], QR[p0:p0 + 64, fc, 0, :].bc(F32), S0P[p0:p0 + 64, fc, :], start=True, stop=False)
                            k.mm(pr[:, h * 64:(h + 1) * 64], AKK[:, h, :], VT_[:, h * 64:(h + 1) * 64], start=False, stop=True)
                        k.cp('act', RH[:, :], pr[:, :])
                        pu = PQa[1]
                        for h in range(16):
                            k.mm(pu[:, h * 64:(h + 1) * 64], XTPTq[h // 4][:, h, 1, :].bc(F32), RH[:, h * 64:(h + 1) * 64])
                        k.ts('dve', NU[:, :], pu[:, :], -1.0, None, ALU.mult)
                        py = PQa[0]
                        for h in range(16):
                            fc, p0 = h // 2, (h % 2) * 64
                            o = py[p0:p0 + 64, fc * 128:(fc + 1) * 128]
                            k.mm(o, S0P[p0:p0 + 64, fc, :], QR[p0:p0 + 64, fc, 1, :].bc(F32), start=True, stop=False)
                            k.mm(o, VT_[:, h * 64:(h + 1) * 64], ARK[:, h, :], start=False, stop=False)
                            k.mm(o, NU[:, h * 64:(h + 1) * 64], ARB[:, h, :], start=False, stop=True)
                        pyv = py[:, :].re("p (c t) -> p c t", t=128)
                        if d_ == 0:
                            k.cp('act', YTt[:, :, :], pyv)
                        else:
                            k.tt('dve', YTt[:, :, :], pyv, YPt[:, :, :], ALU.add)
                        k.ld(featv(YY, 0, 1024, c0, 128), YTt[:, :, :])
                        pss = PQa[1]
                        for fc in range(8):
                            cs = slice(fc * 128, (fc + 1) * 128)
                            k.mm(pss[:, cs], KT_[:, cs], VT_[:, cs], start=True, stop=False)
                            k.mm(pss[:, cs], BT_[:, cs], NU[:, cs], start=False, stop=True)
                        psv = pss[:, :].re("p (c t) -> p c t", t=128)
                        for hp in range(2):
                            blk = slice(hp * 64, (hp + 1) * 64)
                            k.tt('dve', ST_[blk, :, :], psv[blk, :, hp * 64:(hp + 1) * 64], S0P[blk, :, :], ALU.add)
                        k.tt('dve', ST_[:, :, :], ST_[:, :, :], EO[:, :].bcast(2, [128, 8, 64]), ALU.mult)
                    if kind != 'grid':
                        pi = 0 if kind == 'p0' else 1
                        k.ld(ns_rwkv[pi, l, d_, :, :, :], ST_[:, :, :])
        S.barrier()

        with ExitStack() as st:
            XBn = [sb(st, "XBn%d" % i, [128, 8, 515]) for i in range(2)]
            XC = sb(st, "XC", [128, 8, TT], F32R)
            BD = [[sb(st, "BD%d_%d" % (g, d_), [128, 8, 128], F32R) for d_ in range(2)] for g in range(2)]
            RG = sb(st, "RG", [128, 8, TT])
            IG = sb(st, "IG", [128, 8, TT])
            AA = sb(st, "AA", [128, 8, TT])
            UU = sb(st, "UU", [128, 8, TT])
            HH = [sb(st, "HH%d" % i, [128, 8, TT]) for i in range(2)]
            HC = sb(st, "HC", [128, 8])
            CLM = sb(st, "CLM", [128, 16])
            PG = [pt(st, "PG%d" % i, 512) for i in range(4)]
            for g, gw in enumerate((gr_w, gi_w)):
                for d_ in range(2):
                    k.cp('pool', BD[g][d_][:, :, :], ZER[:, :].bcast(1, [128, 8, 128]))
                    for hp in range(2):
                        k.ld(BD[g][d_][hp * 64:(hp + 1) * 64, :, hp * 64:(hp + 1) * 64],
                             gw[l, d_, :, :, :].re("(c a) i j -> a i c j", a=2)[hp], eng='pool')
            k.act(CLM[:, :], vcol('lam_0', l, 0, 16), AF.Exp, scale=-1.0)
            k.act(CLM[:, :], CLM[:, :], AF.Ln, bias=1.0)
            k.ts('dve', CLM[:, :], CLM[:, :], -8.0, None, ALU.mult)
            segs = [(i * TT, TT, 0, T_S, 'grid') for i in range(NTS)] + [(T_S, 256, T_S, T_S + 256, 'p0'),
                                                                          (T_S + 256, 256, T_S + 256, T_S + 512, 'p1')]
            nseg = 0
            for d_ in range(2):
                HOUT = HF if d_ == 0 else HB
                order = segs if d_ == 0 else segs[::-1]
                for (g0, n, q0, q1, kind) in order:
                    xb = XBn[nseg % 2]
                    hh = HH[nseg % 2]
                    nseg += 1
                    lo = g0 - 2 if g0 > q0 else g0
                    hi = g0 + n + 1 if g0 + n < q1 else g0 + n
                    if g0 == q0:
                        k.ms('pool', xb[:, :, 0:2], 0.0)
                    if g0 + n >= q1:
                        k.ms('pool', xb[:, :, 2 + n:3 + n], 0.0)
                    k.ld(xb[:, :, 2 - (g0 - lo):2 + n + (hi - g0 - n)], featv(Z, 4096, 1024, lo, hi - lo))
                    for fc in range(8):
                        e_ = 'pool' if fc % 2 else 'dve'
                        k.ts(e_, UU[:, fc, 0:n], xb[:, fc, 0:n], vcol('conv_w0', l, fc), vcol('conv_b', l, fc), ALU.mult, ALU.add)
                        k.stt(e_, UU[:, fc, 0:n], xb[:, fc, 1:1 + n], vcol('conv_w1', l, fc), UU[:, fc, 0:n], ALU.mult, ALU.add)
                        k.stt(e_, UU[:, fc, 0:n], xb[:, fc, 2:2 + n], vcol('conv_w2', l, fc), UU[:, fc, 0:n], ALU.mult, ALU.add)
                        k.stt(e_, XC[:, fc, 0:n], xb[:, fc, 3:3 + n], vcol('conv_w3', l, fc), UU[:, fc, 0:n], ALU.mult, ALU.add)
                    for fc in range(8):
                        pr_, pi_ = PG[(2 * fc) % 4], PG[(2 * fc + 1) % 4]
                        k.mm(pr_[:, 0:n], BD[0][d_][:, fc, :], XC[:, fc, 0:n])
                        k.mm(pi_[:, 0:n], BD[1][d_][:, fc, :], XC[:, fc, 0:n])
                        k.act(RG[:, fc, 0:n], pr_[:, 0:n], AF.Sigmoid, bias=vcol('gr_b_%d' % d_, l, fc), scale=1.0)
                        k.act(IG[:, fc, 0:n], pi_[:, 0:n], AF.Sigmoid, bias=vcol('gi_b_%d' % d_, l, fc), scale=1.0)
                    for fc in range(8):
                        k.act(AA[:, fc, 0:n], RG[:, fc, 0:n], AF.Exp, scale=CLM[:, d_ * 8 + fc:d_ * 8 + fc + 1])
                    k.tt('pool', RG[:, :, 0:n], AA[:, :, 0:n], AA[:, :, 0:n], ALU.mult)
                    k.ts('dve', RG[:, :, 0:n], RG[:, :, 0:n], -1.0, 1.0, ALU.mult, ALU.add)
                    k.ts('dve', RG[:, :, 0:n], RG[:, :, 0:n], 1e-30, None, ALU.max)
                    k.act(RG[:, :, 0:n], RG[:, :, 0:n], AF.Ln)
                    k.act(RG[:, :, 0:n], RG[:, :, 0:n], AF.Exp, scale=0.5)
                    k.tt('pool', IG[:, :, 0:n], IG[:, :, 0:n], XC[:, :, 0:n].bc(F32), ALU.mult)
                    k.tt('dve', UU[:, :, 0:n], RG[:, :, 0:n], IG[:, :, 0:n], ALU.mult)
                    start_seq = (g0 == q0) if d_ == 0 else (g0 + n >= q1)
                    if start_seq:
                        if kind == 'grid':
                            k.ld(HC[:, :], h0[l, d_, :, :])
                        else:
                            k.ms('pool', HC[:, :], 0.0)
                    for fc in range(8):
                        if d_ == 0:
                            k.scan(hh[:, fc, 0:n], AA[:, fc, 0:n], UU[:, fc, 0:n], HC[:, fc:fc + 1])
                        else:
                            k.scan(hh[:, fc, n - 1::-1] if False else hh[:, fc, 0:n][:, ::-1], AA[:, fc, 0:n][:, ::-1], UU[:, fc, 0:n][:, ::-1], HC[:, fc:fc + 1])
                    k.cp('act', HC[:, :], hh[:, :, n - 1] if d_ == 0 else hh[:, :, 0])
                    k.ld(featv(HOUT, 0, 1024, g0, n), hh[:, :, 0:n])
                    end_seq = (g0 + n >= q1) if d_ == 0 else (g0 == q0)
                    if end_seq and kind != 'grid':
                        k.ld(ns_lru[0 if kind == 'p0' else 1, l, d_, :, :], HC[:, :])
        S.barrier()

        with ExitStack() as st:
            IN3 = [sb(st, "IN3_%d" % i, [128, 8, TT]) for i in range(3)]
            YG = sb(st, "YG", [128, 8, TT], MMDT)
            MG = sb(st, "MG", [128, 16, TT], MMDT)
            MGF = sb(st, "MGF", [128, 16, TT]) if MMDT is not F32R else None

            def mgf(dc_):
                return MGF[:, dc_, :] if MGF is not None else MG[:, dc_, :].bc(F32)
            WS = [sb(st, "WS%d" % i, [128, 4096], MMDT) for i in range(2)]
            SQ = [sb(st, "SQ4%d" % i, [128, TT], F32R) for i in range(2)]
            YC = [sb(st, "YC%d" % i, [128, TT]) for i in range(2)]
            RS = [sb(st, "RS0", [128, TT])] * 2
            MCH = [sb(st, "MCH%d" % i, [128, TT]) for i in range(2)]
            XCH = [sb(st, "XCH%d" % i, [128, TT]) for i in range(3)]
            MAH = [sb(st, "MAH%d" % i, [128, 8, TT]) for i in range(2)]
            XNS = [sb(st, "XNS%d" % i, [128, TT]) for i in range(2)]
            PP = [pt(st, "PP%d" % i, 512) for i in range(6)]
            nw = 0
            nm_ = 0
            for ti, (t0, ci) in enumerate(tiles):
                Yt, Bt, Gt = IN3

                def load_A(tt0):
                    k.ld(Yt[:, :, :], featv(YY, 0, 1024, tt0, TT))
                    k.ld(Bt[:, :, :], featv(BO, 0, 1024, tt0, TT))
                    k.ld(Gt[:, :, :], featv(Z, 3072, 1024, tt0, TT))

                def load_M(br_, hh_, tt0):
                    k.ld(MAH[hh_][:, :, :], featv(Z, 6144 + br_ * 2048 + hh_ * 1024, 1024, tt0, TT))
                if ti == 0:
                    load_A(t0)
                load_M(0, 0, t0)
                load_M(0, 1, t0)
                k.act(Gt[:, :, :], Gt[:, :, :], AF.Silu)
                for fc in range(8):
                    pm_, pv_ = PP[(2 * fc) % 4], PP[(2 * fc + 1) % 4]
                    yc, rs, sq = YC[fc % 2], RS[fc % 2], SQ[fc % 2]
                    k.mm(pm_[:, :], BONE[:, :], Yt[:, fc, :])
                    k.stt('dve', yc[:, :], pm_[:, :], -1.0 / 64, Yt[:, fc, :], ALU.mult, ALU.add)
                    k.act(sq[:, :], yc[:, :], AF.Square)
                    k.mm(pv_[:, :], BONER[:, :], sq[:, :])
                    k.act(rs[:, :], pv_[:, :], AF.Ln, bias=vcol('eps_gn', 0), scale=1.0 / 64)
                    k.act(rs[:, :], rs[:, :], AF.Exp, scale=-0.5)
                    k.tt('pool', yc[:, :], yc[:, :], rs[:, :], ALU.mult)
                    k.ts('pool', yc[:, :], yc[:, :], vcol('lnx_g', l, fc), vcol('lnx_b', l, fc), ALU.mult, ALU.add)
                    k.tt('dve', yc[:, :], yc[:, :], Bt[:, fc, :], ALU.add)
                    k.tt('dve', YG[:, fc, :], yc[:, :], Gt[:, fc, :], ALU.mult)
                k.ld(Yt[:, :, :], featv(HF, 0, 1024, t0, TT))
                k.ld(Bt[:, :, :], featv(HB, 0, 1024, t0, TT))
                k.ld(Gt[:, :, :], featv(Z, 5120, 1024, t0, TT))
                for br, wmat in ((0, w_out_a), (1, w_out_b)):
                    if br == 1:
                        k.act(Gt[:, :, :], Gt[:, :, :], AF.Silu)
                        k.tt('pool', Yt[:, :, :], Yt[:, :, :], Bt[:, :, :], ALU.add)
                        k.tt('dve', YG[:, :, :], Yt[:, :, :], Gt[:, :, :], ALU.mult)
                        if ti + 1 < len(tiles):
                            load_A(tiles[ti + 1][0])
                    for wq in range(4):
                        ws = WS[nw % 2]
                        nw += 1
                        wv = ws[:, :].re("p (k n) -> p k n", k=8)
                        k.ld(wv, wmat[l, :, wq * 512:(wq + 1) * 512].re("(k p) n -> p k n", p=128), eng='pool')
                        for j in range(4):
                            dc = wq * 4 + j
                            ps = PP[4 + dc % 2]
                            for fc in range(8):
                                k.mm(ps[:, :], wv[:, fc, j * 128:(j + 1) * 128], YG[:, fc, :], start=(fc == 0), stop=(fc == 7))
                            mch = MCH[nm_ % 2]
                            nm_ += 1
                            k.act(mch[:, :], MAH[dc // 8][:, dc % 8, :], AF.Sigmoid)
                            if br == 0 and dc % 8 == 7:
                                load_M(1, dc // 8, t0)
                            if br == 0:
                                k.tt('dve', (MGF[:, dc, :] if MGF is not None else MG[:, dc, :]), ps[:, :], mch[:, :], ALU.mult)
                            else:
                                k.tt('dve', mch[:, :], ps[:, :], mch[:, :], ALU.mult)
                                k.tt('pool', MG[:, dc, :], mgf(dc), mch[:, :], ALU.add)
                for oc_ in range(2):
                    k.ld(XCH[oc_][:, :], XIN[oc_ * 128:(oc_ + 1) * 128, t0:t0 + TT])
                for wq in range(8):
                    ws = WS[nw % 2]
                    nw += 1
                    wv = ws[:, :].re("p (k n) -> p k n", k=16)
                    k.ld(wv, w_out[l, :, wq * 256:(wq + 1) * 256].re("(k p) n -> p k n", p=128), eng='pool')
                    for j in range(2):
                        oc = wq * 2 + j
                        ps = PP[4 + oc % 2]
                        for dc in range(16):
                            k.mm(ps[:, :], wv[:, dc, j * 128:(j + 1) * 128], MG[:, dc, :], start=(dc == 0), stop=(dc == 15))
                        xch, xns = XCH[oc % 3], XNS[oc % 2]
                        if oc + 2 < 16:
                            k.ld(XCH[(oc + 2) % 3][:, :], XIN[(oc + 2) * 128:(oc + 3) * 128, t0:t0 + TT])
                        k.stt('dve', xns[:, :], ps[:, :], MOD[:, 32 + oc, ci:ci + 1], xch[:, :], ALU.mult, ALU.add)
                        k.ld(XOUT[oc * 128:(oc + 1) * 128, t0:t0 + TT], xns[:, :])
        S.barrier()

    with ExitStack() as st:
        XB = [sb(st, "XB5%d" % i, [128, 16, TT]) for i in range(2)]
        SQ = [sb(st, "SQ5%d" % i, [128, TT], F32R) for i in range(2)]
        RSTD = sb(st, "RSTD5", [128, TT])
        XNn = sb(st, "XNn", [128, 16, TT])
        YTK = [sb(st, "YTK%d" % i, [128, 4, D]) for i in range(2)]
        PSS = pt(st, "PSS5", 512)
        PT5 = [pt(st, "PT5%d" % i, 512) for i in range(4)]
        for ti, (t0, ci) in enumerate(tiles):
            xb = XB[ti % 2]
            yt = YTK[ti % 2]
            k.ld(xb[:, :, :], featv(XT[L], 0, D, t0, TT))
            rms_to(st, xb, PSS, RSTD, SQ)
            for dc in range(16):
                k.stt('dve' if dc % 2 else 'pool', XNn[:, dc, :], xb[:, dc, :], vcol('final_g', 0, dc), RSTD[:, :], ALU.mult, ALU.mult)
            n5 = 0
            for s_ in range(4):
                for dq in range(4):
                    ps = PT5[n5 % 4]
                    n5 += 1
                    for j in range(4):
                        dc = dq * 4 + j
                        k.tr(ps[:, j * 128:(j + 1) * 128], XNn[:, dc, s_ * 128:(s_ + 1) * 128], IDENT)
                    k.cp('act' if dq % 2 else 'dve', yt[:, s_, dq * 512:(dq + 1) * 512], ps[:, :])
            k.ld(y_all[t0:t0 + TT, :].re("(s p) d -> p s d", p=128), yt[:, :, :])
    S.barrier()
    S.emit(es)
    es.close()
    return nc


def make_consts():
    c = np.zeros((128, 8, 512), np.float32)
    p = np.arange(128)[:, None]
    f = np.arange(128)[None, :]
    ident = (p == f).astype(np.float32)
    bone = ((p // 64) == (f // 64)).astype(np.float32)
    gt = (f > p).astype(np.float32)
    ge = (f >= p).astype(np.float32)
    lt = (f < p).astype(np.float32)
    le = (f <= p).astype(np.float32)
    for i, m in enumerate((ident, bone, gt, ge, lt, le, -gt, -lt)):
        c[:, i, :] = np.tile(m, (1, 4))
    return c


def pack_vecs(inp):
    v = np.zeros((128, NVEC), np.float32)

    def put(name, l, arr):
        a = pm(arr)
        o = VOFF[(name, l)]
        v[:, o:o + a.shape[1]] = a
    for l in range(L):
        put('norm_g', l, inp['norm_g'][l])
        put('ada_b', l, inp['ada_b'][l])
        for j, n in enumerate(('mu_r', 'mu_k', 'mu_v')):
            put(n, l, inp['mu_rkv'][l, j])
        for d_ in range(2):
            put('dec_w0_%d' % d_, l, inp['dec_w0'][l, d_])
            put('iclr_w0_%d' % d_, l, inp['iclr_w0'][l, d_])
            put('gr_b_%d' % d_, l, inp['gr_b'][l, d_])
            put('gi_b_%d' % d_, l, inp['gi_b'][l, d_])
            put('lam_%d' % d_, l, inp['lru_lambda'][l, d_])
        if l > 0:
            put('vres_w0', l, inp['vres_w0'][l - 1])
        put('k_k', l, inp['k_k'][l])
        put('k_a', l, inp['k_a'][l])
        put('r_k', l, inp['r_k'][l].reshape(-1))
        put('lnx_g', l, inp['lnx_g'][l])
        put('lnx_b', l, inp['lnx_b'][l])
        for j in range(4):
            put('conv_w%d' % j, l, inp['conv_w'][l, j])
        put('conv_b', l, inp['conv_b'][l])
    put('final_g', 0, inp['final_g'])
    v[:, VOFF[('eps_rms', 0)]] = RMS_EPS
    v[:, VOFF[('eps_gn', 0)]] = GN_EPS
    return v


WNAMES = ['ada_w', 'w_in', 'w_out_a', 'w_out_b', 'w_out', 'dec_w1', 'dec_w2', 'iclr_w1', 'iclr_w2', 'vres_w1',
          'vres_w2', 'gr_w', 'gi_w']


def run(inp, n_cores, T_S, debug=False):
    inp = {k_: np.asarray(v_) for k_, v_ in inp.items()}
    nc = bass.Bass("TRN2", target_bir_lowering=False)
    build(nc, T_S, debug)
    consts = make_consts()
    vecs = pack_vecs(inp)
    shared = {n: np.ascontiguousarray(inp[n], dtype=np.float32) for n in WNAMES}
    in_maps = []
    for b in range(n_cores):
        xa = np.concatenate([inp['x_sample'][b], inp['x_prompt'][2 * b], inp['x_prompt'][2 * b + 1]], axis=0)
        cond = np.stack([inp['c'][b], inp['c_ctx']], axis=0)
        condT = np.ascontiguousarray(cond.reshape(2, 16, 128).transpose(2, 1, 0))
        st = inp['state_rwkv'][b]
        s0T = np.ascontiguousarray(st.transpose(0, 1, 2, 4, 3).reshape(L, 2, 8, 128, 64))
        h0 = np.ascontiguousarray(inp['state_lru'][b].reshape(L, 2, 8, 128).transpose(0, 1, 3, 2))
        m = dict(x_all=np.ascontiguousarray(xa, dtype=np.float32), condT=condT.astype(np.float32), s0T=s0T.astype(np.float32),
                 h0=h0.astype(np.float32), vecs=vecs, consts=consts)
        m.update(shared)
        in_maps.append(m)
    res = run_bass_kernel_spmd(nc, in_maps, core_ids=list(range(n_cores)))
    return res.results


def kernel(**inputs):
    T_S = 4096
    r = run(inputs, 8, T_S)
    y_prompt = np.zeros((16, 256, D), np.float32)
    y_sample = np.zeros((8, T_S, D), np.float32)
    nsr = np.zeros((16, L, 2, 16, 64, 64), np.float32)
    nsl = np.zeros((16, L, 2, 1024), np.float32)
    for b in range(8):
        ya = r[b]['y_all']
        y_sample[b] = ya[:T_S]
        y_prompt[2 * b] = ya[T_S:T_S + 256]
        y_prompt[2 * b + 1] = ya[T_S + 256:T_S + 512]
        for pi in range(2):
            s = r[b]['ns_rwkv'][pi]
            s = s.reshape(L, 2, 2, 64, 8, 64).transpose(0, 1, 4, 2, 5, 3)
            nsr[2 * b + pi] = s.reshape(L, 2, 16, 64, 64)
            hl = r[b]['ns_lru'][pi]
            nsl[2 * b + pi] = hl.transpose(0, 1, 3, 2).reshape(L, 2, 1024)
    return (y_prompt, y_sample, nsr, nsl)
```
